# Optimizing a Trainium2 kernel written in Bass

```python
import jax, jax.numpy as jnp
from jax import lax
import numpy as np

D_MODEL = 2048
BATCH = 4
SEQ = 4096
DEPTH = 2

N_A_LAYERS = DEPTH // 2
N_B_LAYERS = DEPTH - N_A_LAYERS
POOL_WINDOWS = (2, 4, 8, 16)
N_POOL_GROUPS = len(POOL_WINDOWS)
POOL_GROUP_DIM = D_MODEL // N_POOL_GROUPS
HEAD_DIM = 64
N_Q_HEADS = D_MODEL // HEAD_DIM
N_KV_HEADS = N_Q_HEADS // 8
GQA_GROUP = N_Q_HEADS // N_KV_HEADS
ATTN_WIDTH = N_Q_HEADS * HEAD_DIM
KV_WIDTH = N_KV_HEADS * HEAD_DIM
WINDOW = 128
BLOCK = 128
ROPE_THETA = 10000.0
LN_EPS = 1e-5
NEG_INF = -1e30
DEEPNORM_ALPHA = (2 * DEPTH) ** 0.25
DEEPNORM_BETA = (8 * DEPTH) ** -0.25

kernel_name = "yoco_pool_swa_sink_hybrid"


def layer_norm(x, g, b):
    xf = x.astype(jnp.float32)
    mu = jnp.mean(xf, axis=-1, keepdims=True)
    var = jnp.mean(jnp.square(xf - mu), axis=-1, keepdims=True)
    y = (xf - mu) * lax.rsqrt(var + LN_EPS) * g.astype(jnp.float32) + b.astype(jnp.float32)
    return y.astype(x.dtype)


def rope(t, pos):
    d = t.shape[-1]
    inv_freq = ROPE_THETA ** (-jnp.arange(0, d, 2, dtype=jnp.float32) / d)
    ang = pos[:, None] * inv_freq[None, :]
    ang = jnp.concatenate([ang, ang], axis=-1)[:, None, :]
    tf = t.astype(jnp.float32)
    t1, t2 = tf[..., : d // 2], tf[..., d // 2:]
    rot = jnp.concatenate([-t2, t1], axis=-1)
    return (tf * jnp.cos(ang) + rot * jnp.sin(ang)).astype(t.dtype)


def causal_window_mean(u, w):
    s = u.shape[1]
    c = jnp.cumsum(u, axis=1)
    c_prev = jnp.pad(c, ((0, 0), (w, 0), (0, 0)))[:, :s]
    count = jnp.minimum(jnp.arange(s) + 1, w).astype(jnp.float32)
    return (c - c_prev) / count[None, :, None]


def pool_mixer(x, w_in, w_group, scale, w_out):
    b, s, _ = x.shape
    h = x @ w_in
    u, z = h[..., :D_MODEL], h[..., D_MODEL:]
    ug = u.astype(jnp.float32).reshape(b, s, N_POOL_GROUPS, POOL_GROUP_DIM)
    pooled = jnp.stack(
        [causal_window_mean(ug[:, :, g], w) - ug[:, :, g] for g, w in enumerate(POOL_WINDOWS)],
        axis=2)
    mixed = jnp.einsum('bsgc,gcd->bsgd', pooled.astype(x.dtype), w_group).reshape(b, s, D_MODEL)
    y = mixed * scale * jax.nn.silu(z)
    return y @ w_out


def shared_kv(x, w_k, w_v, pos):
    b, s, _ = x.shape
    k = rope((x @ w_k).reshape(b, s, N_KV_HEADS, HEAD_DIM), pos)
    v = (x @ w_v).reshape(b, s, N_KV_HEADS, HEAD_DIM)
    return k, v


def band(t, nb):
    b = t.shape[0]
    tp = jnp.pad(t, ((0, 0), (BLOCK, 0), (0, 0), (0, 0)))
    tb = tp.reshape(b, nb + 1, BLOCK, t.shape[2], t.shape[3])
    return jnp.concatenate([tb[:, :-1], tb[:, 1:]], axis=2)


def banded_swa_with_sinks(q, k, v, sinks):
    b, s, _, d = q.shape
    nb = s // BLOCK
    qb = q.reshape(b, nb, BLOCK, N_KV_HEADS, GQA_GROUP, d)
    kb, vb = band(k, nb), band(v, nb)
    scores = jnp.einsum('bnqkgd,bnskd->bnkgqs', qb, kb,
                        preferred_element_type=jnp.float32) * (d ** -0.5)
    blk = jnp.arange(nb)[:, None, None]
    q_pos = blk * BLOCK + jnp.arange(BLOCK)[None, :, None]
    k_pos = (blk - 1) * BLOCK + jnp.arange(2 * BLOCK)[None, None, :]
    valid = (k_pos <= q_pos) & (k_pos > q_pos - WINDOW) & (k_pos >= 0)
    scores = jnp.where(valid[None, :, None, None], scores, NEG_INF)
    sink = sinks.astype(jnp.float32).reshape(N_KV_HEADS, GQA_GROUP)[None, None, :, :, None, None]
    m = jnp.maximum(jnp.max(scores, axis=-1, keepdims=True), sink)
    p = jnp.exp(scores - m)
    probs = p / (jnp.sum(p, axis=-1, keepdims=True) + jnp.exp(sink - m))
    out = jnp.einsum('bnkgqs,bnskd->bnqkgd', probs.astype(v.dtype), vb)
    return out.reshape(b, s, N_Q_HEADS * d)


def swa_mixer(x, k, v, w_qg, sinks, w_out, pos):
    b, s, _ = x.shape
    h = x @ w_qg
    q = rope(h[..., :ATTN_WIDTH].reshape(b, s, N_Q_HEADS, HEAD_DIM), pos)
    z = h[..., ATTN_WIDTH:]
    o = banded_swa_with_sinks(q, k, v, sinks)
    return (o * jax.nn.silu(z)) @ w_out


def setup_inputs(seed: int = 0) -> dict:
    key = jax.random.key(seed)
    ks = jax.random.split(key, 16)
    f32 = jnp.float32
    nrm = lambda k, shape, fan_in: jax.random.normal(k, shape, f32) * fan_in ** -0.5
    return {
        "x": jax.random.normal(ks[0], (BATCH, SEQ, D_MODEL), f32),
        "ln_g": 1.0 + 0.02 * jax.random.normal(ks[1], (DEPTH, D_MODEL), f32),
        "ln_b": 0.02 * jax.random.normal(ks[2], (DEPTH, D_MODEL), f32),
        "a_w_in": nrm(ks[3], (N_A_LAYERS, D_MODEL, 2 * D_MODEL), D_MODEL),
        "a_w_group": nrm(ks[4], (N_A_LAYERS, N_POOL_GROUPS, POOL_GROUP_DIM, POOL_GROUP_DIM), POOL_GROUP_DIM),
        "a_scale": 1.0 + 0.02 * jax.random.normal(ks[5], (N_A_LAYERS, D_MODEL), f32),
        "a_w_out": nrm(ks[6], (N_A_LAYERS, D_MODEL, D_MODEL), D_MODEL) * DEEPNORM_BETA,
        "b_w_k": nrm(ks[7], (D_MODEL, KV_WIDTH), D_MODEL),
        "b_w_v": nrm(ks[8], (D_MODEL, KV_WIDTH), D_MODEL) * DEEPNORM_BETA,
        "b_w_qg": nrm(ks[9], (N_B_LAYERS, D_MODEL, 2 * ATTN_WIDTH), D_MODEL),
        "b_sinks": 0.5 * jax.random.normal(ks[10], (N_B_LAYERS, N_Q_HEADS), f32),
        "b_w_out": nrm(ks[11], (N_B_LAYERS, ATTN_WIDTH, D_MODEL), ATTN_WIDTH) * DEEPNORM_BETA,
    }


def reference(x, ln_g, ln_b, a_w_in, a_w_group, a_scale, a_w_out,
              b_w_k, b_w_v, b_w_qg, b_sinks, b_w_out):
    s = x.shape[1]
    pos = jnp.arange(s, dtype=jnp.float32)
    k_sh, v_sh = None, None
    for i in range(DEPTH):
        if i < N_A_LAYERS:
            y = pool_mixer(x, a_w_in[i], a_w_group[i], a_scale[i], a_w_out[i])
        else:
            j = i - N_A_LAYERS
            if j == 0:
                k_sh, v_sh = shared_kv(x, b_w_k, b_w_v, pos)
            y = swa_mixer(x, k_sh, v_sh, b_w_qg[j], b_sinks[j], b_w_out[j], pos)
        x = layer_norm(DEEPNORM_ALPHA * x + y, ln_g[i], ln_b[i])
    return x
```

```python
import numpy as np
from contextlib import ExitStack
import ml_dtypes
import concourse.bass as bass
import concourse.mybir as mybir
from concourse.bass_utils import run_bass_kernel_spmd

F32 = mybir.dt.float32
BF = mybir.dt.bfloat16
AF = mybir.ActivationFunctionType
ALU = mybir.AluOpType

D = 2048
KC = 16
S = 2
TO = 1024
HL = 128
LB = 16
TC = LB + HL + TO
TY = HL + TO
NT = TY // 128
ALPHA = float((2 * 2) ** 0.25)
EPS = 1e-5
NCORES = 8
SAME_ENG_SYNC_MAX = 256
SAME_ENG_ALL = True
STAGE_X = True


GROUP_KEYS = {
    "SC": {"xb", "u", "sA", "sB", "pooled", "sz", "tmp16", "gb", "xt", "h", "xb2", "cs", "KT", "kb", "t1", "t2", "th", "szq", "qr", "pt", "rr", "wv"},
    "R1": {"X", "G", "V"},
}


class Op:
    __slots__ = ("eng", "fn", "deps", "sig", "sigval", "dma", "dmaval", "small", "idx")


class Prog:
    ENGS = ("pe", "act", "dve", "pool", "sp")

    def __init__(self):
        self.q = {e: [] for e in self.ENGS}
        self.last_w = {}
        self.rd_eng = {}
        self.rd_dma = {}
        self.dma_cnt = {}
        self.group_of = {}
        self.group_extra = {}
        self.n = 0

    def barrier(self, group):
        ops = list(self.group_extra.get(group, []))
        for k, g in list(self.group_of.items()):
            if g != group:
                continue
            w = self.last_w.pop(k, None)
            if w is not None:
                ops.append(w)
            ops.extend(self.rd_eng.pop(k, {}).values())
            ops.extend(self.rd_dma.pop(k, []))
            del self.group_of[k]
        best = {}
        out = []
        for o in ops:
            if o.dma:
                out.append(o)
            else:
                b = best.get(o.eng)
                if b is None or o.idx > b.idx:
                    best[o.eng] = o
        out.extend(best.values())
        self.group_extra[group] = out

    def op(self, eng, fn, rd=(), wr=(), dma=None, small=False, group=None, extra=()):
        o = Op()
        o.eng = eng
        o.fn = fn
        o.dma = dma
        o.sig = False
        o.sigval = 0
        o.dmaval = 0
        o.small = small
        o.idx = self.n
        self.n += 1
        deps = {}
        for d in extra:
            deps[id(d)] = d
        if group is not None:
            for d in self.group_extra.get(group, ()):
                deps[id(d)] = d
            for k in list(rd) + list(wr):
                if k[0] in GROUP_KEYS[group]:
                    self.group_of[k] = group
        for k in rd:
            w = self.last_w.get(k)
            if w is not None:
                deps[id(w)] = w
            if k[0] == "ps":
                for r in self.rd_eng.get(k, {}).values():
                    if r.eng != eng:
                        deps[id(r)] = r
        for k in wr:
            w = self.last_w.get(k)
            if w is not None:
                deps[id(w)] = w
            for r in self.rd_eng.get(k, {}).values():
                deps[id(r)] = r
            for r in self.rd_dma.get(k, ()):
                deps[id(r)] = r
        o.deps = [d for d in deps.values() if d is not o]
        for d in o.deps:
            d.sig = True
        for k in rd:
            if dma:
                self.rd_dma.setdefault(k, []).append(o)
            else:
                self.rd_eng.setdefault(k, {})[eng] = o
        for k in wr:
            self.last_w[k] = o
            self.rd_eng[k] = {}
            self.rd_dma[k] = []
        if dma:
            c = self.dma_cnt.get(dma, 0) + 16
            self.dma_cnt[dma] = c
            o.dmaval = c
        self.q[eng].append(o)
        return o

    def users(self, prefix):
        ops = []
        for k, w in self.last_w.items():
            if k[0] == prefix:
                ops.append(w)
        for k, d in self.rd_eng.items():
            if k[0] == prefix:
                ops.extend(d.values())
        for k, l in self.rd_dma.items():
            if k[0] == prefix:
                ops.extend(l)
        return ops

    def finalize(self):
        for e in self.ENGS:
            c = 0
            for o in self.q[e]:
                if o.sig and not o.dma:
                    c += 1
                    o.sigval = c

    def replay(self, ename, eng, csem, dsem):
        waited = {}
        for o in self.q[ename]:
            for d in o.deps:
                if d.dma:
                    key = ("d", d.dma)
                    sem = dsem[d.dma]
                    val = d.dmaval
                else:
                    if d.eng == ename:
                        if ename == "pe" or not (d.small or SAME_ENG_ALL):
                            continue
                    key = ("c", d.eng)
                    sem = csem[d.eng]
                    val = d.sigval
                if waited.get(key, 0) >= val:
                    continue
                eng.wait_ge(sem, val)
                waited[key] = val
            if o.fn is None:
                continue
            ins = o.fn(eng)
            if o.dma:
                ins.then_inc(dsem[o.dma], 16)
            elif o.sig:
                ins.then_inc(csem[ename], 1)


def build_program(debug_stop=None):
    nc = bass.Bass("TRN2", target_bir_lowering=False)
    P = Prog()

    def dram(name, shape, dt, kind="ExternalInput"):
        return nc.dram_tensor(name, list(shape), dt, kind=kind).ap()

    xin = dram("xin", [S, TC, D], F32)
    w_in = dram("w_in", [D, 2 * D], F32)
    w_grp = dram("w_grp", [4, 512, 512], F32)
    w_out0 = dram("w_out0", [D, D], F32)
    w_k = dram("w_k", [D, 256], F32)
    w_v = dram("w_v", [D, 256], F32)
    w_qg = dram("w_qg", [D, 2 * D], F32)
    w_out1 = dram("w_out1", [D, D], F32)
    lngb = dram("lngb", [2, 2 * D], F32)
    scale_col = dram("scale_col", [128, 16], F32)
    sink_col = dram("sink_col", [128, 16], F32)
    invc_in = dram("invc", [S, 128, 64], F32)
    cs_in = dram("cs", [S, 128, 2 * TY], F32)
    m2_in = dram("m2", [S, 128, 640], BF)
    hsw_in = dram("hsw", [128, 128], BF)
    ident_in = dram("ident", [128, 128], BF)
    swp_in = dram("swp", [128, 128], BF)
    ones_in = dram("ones2", [128, 64], BF)
    out = dram("out", [S * TO, D], F32, kind="ExternalOutput")
    x1s = dram("x1s", [S, TY, D], F32, kind="Internal")

    st = ExitStack()

    def sb(name, shape, dt):
        return st.enter_context(nc.sbuf_tensor(name, list(shape), dt))

    XG = sb("XG", [128, KC * TC], BF)
    Y = sb("Y", [128, KC * TY], BF)
    WO = sb("WO", [128, KC * D], BF)
    WS = [sb(f"WS{i}", [128, KC * 256], BF) for i in range(3)]
    SCB = 45056
    SC = sb("SC", [128, SCB // 2], BF)
    ident = sb("identb", [128, 128], BF)
    swp = sb("swpb", [128, 128], BF)
    ones2 = sb("ones2b", [128, 64], BF)
    m2 = sb("m2b", [128, 640], BF)
    hsw = sb("hswb", [128, 128], BF)
    scol = sb("scol", [128, 16], F32)
    sinkc = sb("sinkc", [128, 16], F32)
    exps2 = sb("exps2", [128, 16], F32)
    invc = sb("invcb", [128, 64], F32)
    stt_ = sb("stats", [128, 32], F32)
    epsc = sb("epsc", [128, 1], F32)
    PS = [st.enter_context(nc.psum_tensor(f"ps{i}", [128, 512], F32)) for i in range(8)]

    X3 = XG[:, :].rearrange("p (c t) -> p c t", c=KC)
    Y3 = Y[:, :].rearrange("p (c t) -> p c t", c=KC)
    G3 = XG[:, 0:KC * TO].rearrange("p (c t) -> p c t", c=KC)
    V3 = XG[:, KC * TO:KC * TO + NT * 256].rearrange("p (t n) -> p t n", t=NT)
    WO3 = WO[:, :].rearrange("p (c n) -> p c n", c=KC)
    WS3 = [w[:, :].rearrange("p (c n) -> p c n", c=KC) for w in WS]

    sc_off = [0]

    def carve(nbytes):
        o = sc_off[0]
        assert o % 4 == 0
        sc_off[0] = o + nbytes
        assert sc_off[0] <= SCB, sc_off[0]
        return SC[:, o // 2:(o + nbytes) // 2]

    def carve_f32(n):
        return carve(n * 4).bitcast(F32)

    def carve_bf(n):
        return carve(n * 2)

    PSb = [PS[6][:, :].bitcast(BF), PS[7][:, :].bitcast(BF)]

    jobs = []
    job_slot = {}
    issued = [0]

    def slab_job(src_ap, ncols):
        def fn(slot):
            dst = WS3[slot][:, :, 0:ncols]
            src = src_ap.rearrange("(c p) n -> p c n", p=128)
            P.op("pool", lambda e: e.dma_start(out=dst, in_=src), wr=[("ws", slot)], dma=("ws", slot))
        jobs.append(fn)
        return len(jobs) - 1

    def grp_job(g):
        def fn(slot):
            d = WS[slot][:, 0:2048].rearrange("p (c n) -> p c n", c=4)
            P.op("pool", lambda e: e.dma_start(out=d, in_=w_grp[g].rearrange("(c p) n -> p c n", p=128)),
                 wr=[("ws", slot)], dma=("ws", slot))
        jobs.append(fn)
        return len(jobs) - 1

    def k_job(half):
        def fn(slot):
            for jj in range(2):
                j = half * 2 + jj
                for r in range(2):
                    P.op("pool", lambda e, jj=jj, j=j, r=r: e.dma_start(
                        out=WS3[slot][:, :, jj * 128 + r * 64:jj * 128 + (r + 1) * 64],
                        in_=w_k[:, j * 64:(j + 1) * 64].rearrange("(c p) n -> p c n", p=128)), wr=[("ws", slot)], dma=("ws", slot))
        jobs.append(fn)
        return len(jobs) - 1

    def ensure(k):
        while issued[0] <= min(k, len(jobs) - 1):
            i = issued[0]
            slot = i % 3
            job_slot[i] = slot
            jobs[i](slot)
            issued[0] += 1

    def use(k, ahead=2):
        ensure(k + ahead)
        return job_slot[k]

    JL0 = []
    JL1 = []
    for s_ in range(S):
        d0 = {}
        for g in range(4):
            for j in range(2):
                d0[(g, "u", j)] = slab_job(w_in[:, (2 * g + j) * 256:(2 * g + j + 1) * 256], 256)
            for j in range(2):
                d0[(g, "z", j)] = slab_job(w_in[:, D + (2 * g + j) * 256:D + (2 * g + j + 1) * 256], 256)
            d0[(g, "w")] = grp_job(g)
        JL0.append(d0)
        d1 = {}
        d1["K"] = slab_job(w_k[:, :], 256)
        d1["V"] = slab_job(w_v[:, :], 256)
        for p_ in range(8):
            d1[("q", p_)] = slab_job(w_qg[:, p_ * 256:(p_ + 1) * 256], 256)
            d1[("z", p_)] = slab_job(w_qg[:, D + p_ * 256:D + (p_ + 1) * 256], 256)
        JL1.append(d1)

    def mm_group(out_ap, pairs, rd, wr, **kw):
        n = len(pairs)
        last = None
        for j, (l, r) in enumerate(pairs):
            last = P.op("pe", lambda e, o=out_ap, l=l, r=r, a=(j == 0), b=(j == n - 1): e.matmul(o, l, r, start=a, stop=b, **kw),
                        rd=rd if j == n - 1 else (), wr=wr if j == n - 1 else ())
        return last

    def mm_group2(out_ap, pairs, rd, wr, **kw):
        n = len(pairs)
        for j, (l, r) in enumerate(pairs):
            P.op("pe", lambda e, o=out_ap, l=l, r=r, a=(j == 0), b=(j == n - 1): e.matmul(o, l, r, start=a, stop=b, **kw),
                 rd=rd if (j == 0 or j == n - 1) else (), wr=wr if (j == 0 or j == n - 1) else ())

    def cload(dst, src, key):
        P.op("sp", lambda e: e.dma_start(out=dst, in_=src), wr=[key], dma=key)

    cload(ident[:, :], ident_in, ("c", "ident"))
    cload(swp[:, :], swp_in, ("c", "swp"))
    cload(hsw[:, :], hsw_in, ("c", "hsw"))
    cload(ones2[:, :], ones_in, ("c", "ones"))
    cload(scol[:, :], scale_col, ("c", "scol"))
    cload(sinkc[:, :], sink_col, ("c", "sink"))
    P.op("dve", lambda e: e.memset(epsc[:, :], EPS), wr=[("c", "eps")], small=True)
    P.op("act", lambda e: e.activation(out=exps2[:, :], in_=sinkc[:, :], func=AF.Exp), rd=[("c", "sink")], wr=[("c", "exps")], small=True)
    P.op("dve", lambda e: e.tensor_scalar(out=exps2[:, :], in0=exps2[:, :], scalar1=2.0, scalar2=None, op0=ALU.mult),
         rd=[("c", "exps")], wr=[("c", "exps")], small=True)

    def xkeys(a, b):
        ks = []
        if a < LB:
            ks.append(("X", "lb"))
        for t in range(NT):
            lo, hi = LB + t * 128, LB + (t + 1) * 128
            if a < hi and b > lo:
                ks.append(("X", t))
        return ks

    def ykeys(a, b):
        return [("Y", t) for t in range(NT) if a < (t + 1) * 128 and b > t * 128]

    acc_ring = {"i": 0}

    def next_acc(banks):
        b = banks[acc_ring["i"] % len(banks)]
        acc_ring["i"] += 1
        return b

    staged = [False]
    lbst = [None]
    for s in range(S):
        P.barrier("R1")
        P.barrier("SC")
        sc_off[0] = 0
        NXB = 6
        xb = [carve_bf(D) for _ in range(NXB)]
        P.op("sp", lambda e, s=s: e.dma_start(out=m2[:, :], in_=m2_in[s]), wr=[("c", "m2")], dma=("c", "m2"))
        P.op("sp", lambda e, s=s: e.dma_start(out=invc[:, :], in_=invc_in[s]), wr=[("c", "invc")], dma=("c", "invc"))

        def transposes(src, rows, tkey, dst3, col0, ncol, dst_key, grp, bset=1, extra=()):
            for half in range(2):
                bank = PS[4 + 2 * bset + half]
                bkey = ("ps", id(bank))
                pb = bank[:, :].bitcast(BF)
                for cc in range(8):
                    c = half * 8 + cc
                    P.op("pe", lambda e, c=c, cc=cc, pb=pb: e.transpose(
                        pb[:, cc * ncol:(cc + 1) * ncol], src[0:rows, c * 128:(c + 1) * 128], ident[0:rows, 0:rows]),
                        rd=[tkey, ("c", "ident")] if cc in (0, 7) else (), wr=[bkey] if cc in (0, 7) else (),
                        extra=extra if cc == 0 else ())
                eng = "act" if half == 0 else "dve"
                src_ps = pb[:, 0:8 * ncol].rearrange("p (c t) -> p c t", c=8)
                dst = dst3[:, half * 8:(half + 1) * 8, col0:col0 + ncol]
                if eng == "act":
                    P.op("act", lambda e, d=dst, s_=src_ps: e.activation(out=d, in_=s_, func=AF.Copy), rd=[bkey], wr=[dst_key], group=grp)
                else:
                    P.op("dve", lambda e, d=dst, s_=src_ps: e.tensor_copy(out=d, in_=s_), rd=[bkey], wr=[dst_key], group=grp)

        ensure(JL0[s][(0, "u", 0)] + 2)
        if staged[0]:
            transposes(lbst[0], LB, ("lbst",), X3, 0, LB, ("X", "lb"), "R1")
        else:
            P.op("pool", lambda e, s=s: e.dma_start(out=xb[0][0:LB, :], in_=xin[s, 0:LB, :]), wr=[("xb", 0)], dma=("xb", 0), group="SC")
            transposes(xb[0], LB, ("xb", 0), X3, 0, LB, ("X", "lb"), "R1")
        stage_users = []
        for t in range(NT):
            if staged[0]:
                ys = Y[:, t * D:(t + 1) * D]
                transposes(ys, 128, ("ystage", t), X3, LB + t * 128, 128, ("X", t), "R1", bset=t % 2)
                stage_users = [P.q["pe"][-1]]
            else:
                sl = (t + 1) % NXB
                P.op("pool", lambda e, s=s, t=t, sl=sl: e.dma_start(out=xb[sl][:, :], in_=xin[s, LB + t * 128:LB + (t + 1) * 128, :]),
                     wr=[("xb", sl)], dma=("xb", sl), group="SC")
                transposes(xb[sl], 128, ("xb", sl), X3, LB + t * 128, 128, ("X", t), "R1", bset=t % 2)
        staged[0] = False

        if debug_stop == "pro":
            break
        P.barrier("SC")
        sc_off[0] = 0
        ubuf = [carve_f32(TC), carve_f32(TC)]
        sA = carve_f32(TC)
        sBf = carve_f32(TC)
        pooled = carve_bf(4 * TY).rearrange("p (c t) -> p c t", c=4)
        szb = carve_bf(4 * TY).rearrange("p (c t) -> p c t", c=4)
        tmp16 = carve_f32(16)

        ublocks = [(0, LB), (LB, LB + 512), (LB + 512, LB + 1024), (LB + 1024, TC)]
        yblocks = [(0, 512), (512, 1024), (1024, TY)]
        ACC0 = [PS[0], PS[1], PS[2], PS[3]]
        GRP0 = [PS[4], PS[5]]
        WIN = (2, 4, 8, 16)
        for g in range(4):
            P.op("pool", lambda e, n=g: e.dma_start(out=WO3[:, :, n * 512:(n + 1) * 512],
                                                    in_=w_out0[:, n * 512:(n + 1) * 512].rearrange("(c p) n -> p c n", p=128)),
                 wr=[("WO", g)], dma=("WO", g))
            for m in range(4):
                c = 4 * g + m
                wsl = use(JL0[s][(g, "u", m // 2)])
                us = c % 2
                for (a, b) in ublocks:
                    bank = next_acc(ACC0)
                    bk = ("ps", id(bank))
                    mm_group2(bank[:, 0:b - a], [(WS3[wsl][:, kc, (m % 2) * 128:(m % 2 + 1) * 128], X3[:, kc, a:b]) for kc in range(KC)],
                              rd=[("ws", wsl)] + xkeys(a, b), wr=[bk])
                    P.op("act", lambda e, bank=bank, a=a, b=b, us=us: e.activation(out=ubuf[us][:, a:b], in_=bank[:, 0:b - a], func=AF.Copy),
                         rd=[bk], wr=[("u", us)], group="SC", small=(b - a) <= SAME_ENG_SYNC_MAX)
                w = WIN[g]
                u = ubuf[us]
                uk = ("u", us)
                P.op("dve", lambda e, u=u: e.tensor_tensor(out=sA[:, 1:TC], in0=u[:, 1:TC], in1=u[:, 0:TC - 1], op=ALU.add),
                     rd=[uk], wr=[("sA",)], group="SC")
                fin, fk = sA, ("sA",)
                if w >= 4:
                    P.op("dve", lambda e: e.tensor_tensor(out=sBf[:, 3:TC], in0=sA[:, 3:TC], in1=sA[:, 1:TC - 2], op=ALU.add),
                         rd=[("sA",)], wr=[("sB",)], group="SC")
                    fin, fk = sBf, ("sB",)
                if w >= 8:
                    P.op("dve", lambda e: e.tensor_tensor(out=sA[:, 7:TC], in0=sBf[:, 7:TC], in1=sBf[:, 3:TC - 4], op=ALU.add),
                         rd=[("sB",)], wr=[("sA",)], group="SC")
                    fin, fk = sA, ("sA",)
                if w >= 16:
                    P.op("dve", lambda e: e.tensor_tensor(out=sBf[:, 15:TC], in0=sA[:, 15:TC], in1=sA[:, 7:TC - 8], op=ALU.add),
                         rd=[("sA",)], wr=[("sB",)], group="SC")
                    fin, fk = sBf, ("sB",)
                P.op("dve", lambda e, fin=fin, u=u, m=m, w=w: e.scalar_tensor_tensor(
                    out=pooled[:, m, :], in0=fin[:, LB:TC], scalar=1.0 / w, in1=u[:, LB:TC], op0=ALU.mult, op1=ALU.subtract),
                    rd=[fk, uk], wr=[("pooled", m)], group="SC")
                o0 = LB + HL
                P.op("dve", lambda e, fin=fin, g=g: e.tensor_tensor(out=tmp16[:, :], in0=fin[:, o0:o0 + 16], in1=invc[:, g * 16:(g + 1) * 16], op=ALU.mult),
                     rd=[fk, ("c", "invc")], wr=[("tmp16",)], group="SC", small=True)
                P.op("dve", lambda e, u=u, m=m: e.tensor_tensor(out=pooled[:, m, HL:HL + 16], in0=tmp16[:, :], in1=u[:, o0:o0 + 16], op=ALU.subtract),
                     rd=[("tmp16",), uk], wr=[("pooled", m)], group="SC", small=True)
            for m in range(4):
                wsl = use(JL0[s][(g, "z", m // 2)])
                for (a, b) in ublocks[1:]:
                    bank = next_acc(ACC0)
                    bk = ("ps", id(bank))
                    mm_group2(bank[:, 0:b - a], [(WS3[wsl][:, kc, (m % 2) * 128:(m % 2 + 1) * 128], X3[:, kc, a:b]) for kc in range(KC)],
                              rd=[("ws", wsl)] + xkeys(a, b), wr=[bk])
                    P.op("act", lambda e, bank=bank, a=a, b=b, m=m: e.activation(out=szb[:, m, a - LB:b - LB], in_=bank[:, 0:b - a], func=AF.Silu),
                         rd=[bk], wr=[("sz", m)], group="SC")
            gi = use(JL0[s][(g, "w")])
            wg3 = WS[gi][:, 0:2048].rearrange("p (c n) -> p c n", c=4)
            for m in range(4):
                c = 4 * g + m
                for bi, (a, b) in enumerate(yblocks):
                    bank = GRP0[(m * 3 + bi) % 2]
                    bk = ("ps", id(bank))
                    mm_group2(bank[:, 0:b - a], [(wg3[:, kc, m * 128:(m + 1) * 128], pooled[:, kc, a:b]) for kc in range(4)],
                              rd=[("ws", gi)] + [("pooled", k) for k in range(4)], wr=[bk])
                    P.op("dve", lambda e, bank=bank, a=a, b=b, m=m, c=c: e.scalar_tensor_tensor(
                        out=Y3[:, c, a:b], in0=bank[:, 0:b - a], scalar=scol[:, c:c + 1], in1=szb[:, m, a:b], op0=ALU.mult, op1=ALU.mult),
                        rd=[bk, ("sz", m), ("c", "scol")], wr=ykeys(a, b), extra=stage_users)

        if debug_stop == "l0p1":
            break
        def phase2(layer, act3, ntiles, tile_col0, res_src, dst_fn, do_transpose):
            P.barrier("SC")
            sc_off[0] = 0
            gb = carve_f32(2 * D)
            xt1 = carve_f32(D)
            xt = [xt1, xt1]
            hh = [carve_f32(D), carve_f32(D)]
            xb2 = carve_bf(D)
            if layer == 1 and s + 1 < S and STAGE_X:
                lbst[0] = xb2
                P.op("pool", lambda e, s=s: e.dma_start(out=xb2[0:LB, :], in_=xin[s + 1, 0:LB, :]), wr=[("lbst",)], dma=("lbst",), extra=P.group_extra.get("SC", []))
            P.op("sp", lambda e: e.dma_start(out=gb[:, :], in_=lngb[layer:layer + 1, :].partition_broadcast(128)),
                 wr=[("gb",)], dma=("gb",), group="SC")
            OUTB = [PS[0], PS[1], PS[2], PS[3]]

            casts = {}

            def tail(t, extra=()):
                transposes(xb2, 128, ("xb2",), Y3, t * 128, 128, ("Y", t), None, extra=extra)

            for t in range(ntiles):
                xs = 0
                h = hh[t % 2]
                hs = t % 2
                if t == 0:
                    P.op("sp", lambda e, t=t, xs=xs: e.dma_start(out=xt[xs][:, :], in_=res_src(t)), rd=[("x1s", s, t)] if layer == 1 else (),
                         wr=[("xt", xs)], dma=("xt", xs), group="SC")
                c0 = tile_col0 + t * 128
                akeys = ykeys(c0, c0 + 128) if layer == 0 else [("G", kc) for kc in range(KC)]
                for n in range(4):
                    bk = ("ps", id(OUTB[n]))
                    mm_group2(OUTB[n][:, :], [(act3[:, kc, c0:c0 + 128], WO3[:, kc, n * 512:(n + 1) * 512]) for kc in range(KC)],
                              rd=akeys + [("WO", n)], wr=[bk])
                    P.op("dve", lambda e, n=n, xs=xs, h=h: e.scalar_tensor_tensor(
                        out=h[:, n * 512:(n + 1) * 512], in0=xt[xs][:, n * 512:(n + 1) * 512], scalar=ALPHA, in1=OUTB[n][:, :],
                        op0=ALU.mult, op1=ALU.add), rd=[bk, ("xt", xs)], wr=[("h", hs, n)], group="SC")
                    P.op("dve", lambda e, n=n, h=h: e.bn_stats(out=stt_[:, n * 6:(n + 1) * 6], in_=h[:, n * 512:(n + 1) * 512]),
                         rd=[("h", hs, n)], wr=[("st", n)], small=True)
                if t + 1 < ntiles:
                    P.op("sp", lambda e, t=t, xs=xs: e.dma_start(out=xt[xs][:, :], in_=res_src(t + 1)), rd=[("x1s", s, t + 1)] if layer == 1 else (),
                         wr=[("xt", xs)], dma=("xt", xs), group="SC")
                if do_transpose and t > 0:
                    tail(t - 1)
                hk = [("h", hs, n) for n in range(4)]
                P.op("dve", lambda e: e.bn_aggr(out=stt_[:, 24:26], in_=stt_[:, 0:24]), rd=[("st", n) for n in range(4)], wr=[("mv",)], small=True)
                P.op("act", lambda e: e.activation(out=stt_[:, 28:29], in_=stt_[:, 25:26], func=AF.Sqrt, bias=epsc[:, 0:1], scale=1.0),
                     rd=[("mv",), ("c", "eps")], wr=[("sd",)], small=True)
                P.op("dve", lambda e: e.reciprocal(out=stt_[:, 26:27], in_=stt_[:, 28:29]), rd=[("sd",)], wr=[("rs",)], small=True)
                P.op("dve", lambda e: e.scalar_tensor_tensor(out=stt_[:, 27:28], in0=stt_[:, 24:25], scalar=-1.0, in1=stt_[:, 26:27],
                                                             op0=ALU.mult, op1=ALU.mult), rd=[("mv",), ("rs",)], wr=[("nmr",)], small=True)
                P.op("act", lambda e, h=h: e.activation(out=h[:, :], in_=h[:, :], func=AF.Identity, bias=stt_[:, 27:28], scale=stt_[:, 26:27]),
                     rd=hk + [("rs",), ("nmr",)], wr=hk, group="SC")
                hkA = [("h", hs, 0), ("h", hs, 1)]
                hkB = [("h", hs, 2), ("h", hs, 3)]
                H2 = D // 2
                P.op("pool", lambda e, h=h: e.tensor_tensor(out=h[:, 0:H2], in0=h[:, 0:H2], in1=gb[:, 0:H2], op=ALU.mult), rd=hkA + [("gb",)], wr=hkA, group="SC")
                P.op("dve", lambda e, h=h: e.tensor_tensor(out=h[:, H2:D], in0=h[:, H2:D], in1=gb[:, H2:D], op=ALU.mult), rd=hkB + [("gb",)], wr=hkB, group="SC")
                P.op("dve", lambda e, h=h: e.tensor_tensor(out=h[:, H2:D], in0=h[:, H2:D], in1=gb[:, D + H2:2 * D], op=ALU.add), rd=hkB + [("gb",)], wr=hkB, group="SC")
                P.op("dve", lambda e, h=h: e.tensor_tensor(out=h[:, 0:H2], in0=h[:, 0:H2], in1=gb[:, D:D + H2], op=ALU.add), rd=hkA + [("gb",)], wr=hkA, group="SC")
                dst, dkey = dst_fn(t)
                if dst is not None:
                    P.op("sp", lambda e, dst=dst, h=h: e.dma_start(out=dst, in_=h[:, :]), rd=hk, wr=[dkey], dma=("hout", hs), group="SC")
                if do_transpose:
                    casts[t] = P.op("act", lambda e, h=h: e.activation(out=xb2[:, :], in_=h[:, :], func=AF.Copy), rd=hk, wr=[("xb2",)], group="SC")
            if do_transpose:
                return lambda: tail(ntiles - 1, extra=[casts[ntiles - 1]])
            return None

        last_tail = phase2(0, Y3, NT, 0,
                           lambda t, s=s: xin[s, LB + t * 128:LB + (t + 1) * 128, :],
                           lambda t, s=s: (x1s[s, t * 128:(t + 1) * 128, :], ("x1s", s, t)),
                           True)
        if debug_stop == "l0p2":
            last_tail()

        if debug_stop == "l0p2":
            break
        P.barrier("SC")
        P.barrier("R1")
        sc_off[0] = 0
        cs = carve_f32(2 * TY)
        KT2 = carve_bf(4 * TY).rearrange("p (c t) -> p c t", c=4)
        kb = carve_bf(TO)
        qr = [carve_bf(TO), carve_bf(TO)]
        t1 = carve_f32(TO)
        t2 = carve_f32(512)
        th = carve_bf(TO)
        szq = [carve_bf(TO), carve_bf(TO)]
        ptb = [[carve_bf(512), carve_bf(512)] for _ in range(3)]
        rr = carve_f32(256)
        wv = carve_f32(256)
        P.op("sp", lambda e, s=s: e.dma_start(out=cs[:, :], in_=cs_in[s]), wr=[("cs",)], dma=("cs",), group="SC")
        ACC1 = [PS[0], PS[1], PS[4], PS[5]]
        cosT = cs[:, 0:TY]
        sinT = cs[:, TY:2 * TY]

        def rope_block(bank, bk, col_a, col_b, dst_ap, dst_key, n, grp):
            P.op("act", lambda e: e.activation(out=kb[:, 0:n], in_=bank[:, 0:n], func=AF.Copy), rd=[bk], wr=[("kb", 0), ("kb", 1)], group="SC")
            P.op("dve", lambda e: e.tensor_tensor(out=t1[:, 0:n], in0=bank[:, 0:n], in1=cosT[:, col_a:col_b], op=ALU.mult),
                 rd=[bk, ("cs",)], wr=[("t1", 0), ("t1", 1)], group="SC")
            rb = next_acc(ACC1)
            rk = ("ps", id(rb))
            P.op("pe", lambda e: e.matmul(rb[:, 0:n], swp[:, :], kb[:, 0:n], start=True, stop=True), rd=[("kb", 0), ("kb", 1), ("c", "swp")], wr=[rk])
            P.op("dve", lambda e: e.tensor_tensor(out=t2[:, 0:n], in0=rb[:, 0:n], in1=sinT[:, col_a:col_b], op=ALU.mult),
                 rd=[rk, ("cs",)], wr=[("t2",)], group="SC")
            P.op("pool", lambda e: e.tensor_tensor(out=dst_ap, in0=t1[:, 0:n], in1=t2[:, 0:n], op=ALU.add),
                 rd=[("t1", 0), ("t1", 1), ("t2",)], wr=[dst_key], group=grp)

        ki = use(JL1[s]["K"])
        vi = use(JL1[s]["V"], ahead=1)
        kunits = [(jc, a, b) for jc in range(2) for (a, b) in yblocks]
        kstate = {}

        def kA(u):
            jc, a, b = kunits[u]
            n = b - a
            bank = next_acc(ACC1)
            bk = ("ps", id(bank))
            mm_group2(bank[:, 0:n], [(WS3[ki][:, kc, jc * 128:(jc + 1) * 128], Y3[:, kc, a:b]) for kc in range(KC)],
                      rd=[("ws", ki)] + ykeys(a, b), wr=[bk])
            P.op("act", lambda e: e.activation(out=kb[:, 0:n], in_=bank[:, 0:n], func=AF.Copy), rd=[bk], wr=[("kb", 0), ("kb", 1)], group="SC")
            P.op("dve", lambda e: e.tensor_tensor(out=t1[:, 0:n], in0=bank[:, 0:n], in1=cosT[:, a:b], op=ALU.mult),
                 rd=[bk, ("cs",)], wr=[("t1", 0), ("t1", 1)], group="SC")

        def kB(u):
            jc, a, b = kunits[u]
            n = b - a
            rb = next_acc(ACC1)
            rk = ("ps", id(rb))
            P.op("pe", lambda e: e.matmul(rb[:, 0:n], swp[:, :], kb[:, 0:n], start=True, stop=True), rd=[("kb", 0), ("kb", 1), ("c", "swp")], wr=[rk])
            P.op("dve", lambda e: e.tensor_tensor(out=t2[:, 0:n], in0=rb[:, 0:n], in1=sinT[:, a:b], op=ALU.mult),
                 rd=[rk, ("cs",)], wr=[("t2",)], group="SC")
            P.op("pool", lambda e: e.tensor_tensor(out=KT2[:, 2 * jc, a:b], in0=t1[:, 0:n], in1=t2[:, 0:n], op=ALU.add),
                 rd=[("t1", 0), ("t1", 1), ("t2",)], wr=[("KT", 2 * jc, a)], group="SC")

        def kS(u):
            jc, a, b = kunits[u]
            n = b - a
            sbk = next_acc(ACC1)
            sk_ = ("ps", id(sbk))
            P.op("pe", lambda e: e.matmul(sbk[:, 0:n], hsw[:, :], KT2[:, 2 * jc, a:b], start=True, stop=True),
                 rd=[("KT", 2 * jc, a), ("c", "hsw")], wr=[sk_])
            P.op("act", lambda e: e.activation(out=KT2[:, 2 * jc + 1, a:b], in_=sbk[:, 0:n], func=AF.Copy),
                 rd=[sk_], wr=[("KT", 2 * jc + 1, a)], group="SC")

        def vT(t):
            bank = next_acc(ACC1)
            bk = ("ps", id(bank))
            mm_group2(bank[:, 0:256], [(Y3[:, kc, t * 128:(t + 1) * 128], WS3[vi][:, kc, 0:256]) for kc in range(KC)],
                      rd=[("ws", vi), ("Y", t)], wr=[bk])
            P.op("act", lambda e: e.activation(out=V3[:, t, :], in_=bank[:, 0:256], func=AF.Copy), rd=[bk], wr=[("V", t)], group="R1")

        def do_tail(i_):
            last_tail()
            P.group_extra.setdefault("SC", []).append(P.q["pe"][-1])

        seq = [("A", 0), ("V", 0), ("V", 1), ("T", 0), ("B", 0), ("A", 1), ("V", 2), ("S", 0), ("B", 1), ("A", 2), ("V", 3), ("S", 1), ("B", 2),
               ("A", 3), ("V", 4), ("S", 2), ("B", 3), ("A", 4), ("V", 5), ("S", 3), ("B", 4), ("A", 5), ("V", 6), ("S", 4), ("B", 5), ("V", 7),
               ("S", 5), ("V", 8)]
        for kind, idx in seq:
            {"A": kA, "B": kB, "S": kS, "V": vT, "T": do_tail}[kind](idx)
        if debug_stop == "l1v":
            break
        SCR = [(PS[2], PS[3]), (PS[2], PS[3])]
        ODB = [PS[6], PS[7]]
        qblocks = [(HL, HL + 512), (HL + 512, HL + 1024)]
        pt_ring = {"i": 0}
        od_ring = {"i": 0}
        slabs_of = {}

        def slabs(c):
            p_ = c // 2
            if p_ not in slabs_of:
                sq = use(JL1[s][("q", p_)])
                slabs_of[p_] = (sq, job_slot[JL1[s][("z", p_)]])
            return slabs_of[p_]

        def q_blk(c, bi):
            sq, _ = slabs(c)
            co = (c % 2) * 128
            a, b = qblocks[bi]
            bank = next_acc(ACC1)
            bk = ("ps", id(bank))
            mm_group2(bank[:, 0:512], [(WS3[sq][:, kc, co:co + 128], Y3[:, kc, a:b]) for kc in range(KC)],
                      rd=[("ws", sq)] + ykeys(a, b), wr=[bk])
            P.op("act", lambda e: e.activation(out=kb[:, bi * 512:(bi + 1) * 512], in_=bank[:, :], func=AF.Copy),
                 rd=[bk], wr=[("kb", bi)], group="SC")
            P.op("dve", lambda e: e.tensor_tensor(out=t1[:, bi * 512:(bi + 1) * 512], in0=bank[:, :], in1=cosT[:, a:b], op=ALU.mult),
                 rd=[bk, ("cs",)], wr=[("t1", bi)], group="SC")
            if c % 2 == 1 and bi == 1:
                ensure(JL1[s][("q", c // 2)] + 3)

        def r_blk(c, bi):
            a, b = qblocks[bi]
            qs = c % 2
            rb = next_acc(ACC1)
            rk = ("ps", id(rb))
            P.op("pe", lambda e: e.matmul(rb[:, :], swp[:, :], kb[:, bi * 512:(bi + 1) * 512], start=True, stop=True),
                 rd=[("kb", bi), ("c", "swp")], wr=[rk])
            P.op("dve", lambda e: e.tensor_tensor(out=t2[:, :], in0=rb[:, :], in1=sinT[:, a:b], op=ALU.mult),
                 rd=[rk, ("cs",)], wr=[("t2",)], group="SC")
            P.op("pool", lambda e: e.tensor_tensor(out=qr[qs][:, bi * 512:(bi + 1) * 512], in0=t1[:, bi * 512:(bi + 1) * 512], in1=t2[:, :], op=ALU.add),
                 rd=[("t1", bi), ("t2",)], wr=[("qr", qs, bi)], group="SC")

        def z_blk(c, bi):
            _, sz_ = slabs(c)
            co = (c % 2) * 128
            a, b = qblocks[bi]
            qs = c % 2
            bank = next_acc(ACC1)
            bk = ("ps", id(bank))
            mm_group2(bank[:, 0:512], [(WS3[sz_][:, kc, co:co + 128], Y3[:, kc, a:b]) for kc in range(KC)],
                      rd=[("ws", sz_)] + ykeys(a, b), wr=[bk])
            P.op("act", lambda e: e.activation(out=th[:, bi * 512:(bi + 1) * 512], in_=bank[:, :], func=AF.Tanh, scale=0.5),
                 rd=[bk], wr=[("th", bi)], group="SC")
            P.op("dve", lambda e: e.scalar_tensor_tensor(
                out=szq[qs][:, bi * 512:(bi + 1) * 512], in0=th[:, bi * 512:(bi + 1) * 512], scalar=1.0, in1=bank[:, :], op0=ALU.add, op1=ALU.mult),
                rd=[bk, ("th", bi)], wr=[("szq", qs, bi)], group="SC")

        def attn_steps(c):
            qs = c % 2
            kvh = c // 4
            stt = {"pt": {}, "od": None}

            def S(pi):
                kts = [k for k in (2 * pi, 2 * pi + 1) if k <= 8]
                banks = SCR[pi % 2]
                for kt in kts:
                    lo = max(kt - 1, 0) * 128
                    hi = min(kt + 1, 8) * 128
                    off = (kt % 2) * 256
                    a_ = off + (0 if kt >= 1 else 128)
                    b_ = off + (256 if kt <= 7 else 128)
                    for e_ in range(2):
                        bk = ("ps", id(banks[e_]))
                        qk = [("qr", qs, bi) for bi in range(2) if lo < (bi + 1) * 512 and hi > bi * 512]
                        ksel = 2 * (kvh // 2) + (0 if kvh % 2 == e_ else 1)
                        P.op("pe", lambda e, e_=e_, kt=kt, lo=lo, hi=hi, a_=a_, b_=b_, bank=banks[e_], ksel=ksel: e.matmul(
                            bank[:, a_:b_], KT2[64 * e_:64 * e_ + 64, ksel, kt * 128:(kt + 1) * 128],
                            qr[qs][64 * e_:64 * e_ + 64, lo:hi], start=True, stop=True), rd=[("KT", ksel, ya) for (ya, yb_) in yblocks if ya < (kt + 1) * 128 and yb_ > kt * 128] + qk, wr=[bk])
                va = 128 if pi == 0 else 0
                vb = 128 if pi == 4 else 512
                ri = pt_ring["i"] % 3
                pt_ring["i"] += 1
                stt["pt"][pi] = ri
                for e_ in range(2):
                    bk = ("ps", id(banks[e_]))
                    pk = ("pt", ri, e_)
                    P.op("act", lambda e, e_=e_, bank=banks[e_]: e.activation(
                        out=ptb[ri][e_][:, va:vb], in_=bank[:, va:vb], func=AF.Exp, scale=0.125), rd=[bk], wr=[pk], group="SC")
                    if pi == 0:
                        segs = [(128, 256, 512), (256, 512, 256)]
                    else:
                        segs = [(va, vb, va)]
                    for (xa, xb_, mo) in segs:
                        P.op("pool", lambda e, e_=e_, xa=xa, xb_=xb_, mo=mo: e.tensor_tensor(
                            out=ptb[ri][e_][:, xa:xb_], in0=ptb[ri][e_][:, xa:xb_], in1=m2[:, mo:mo + xb_ - xa], op=ALU.mult),
                            rd=[pk, ("c", "m2")], wr=[pk], group="SC")

            def PV(pi):
                for qt in (2 * pi, 2 * pi + 1):
                    if qt < 1 or qt > 8:
                        continue
                    r = (qt - 1) % 2
                    if r == 0:
                        stt["od"] = ODB[od_ring["i"] % 2]
                        od_ring["i"] += 1
                    od = stt["od"]
                    odk = ("ps", id(od))
                    pprev = stt["pt"][(qt - 1) // 2]
                    pcur = stt["pt"][qt // 2]
                    for (dst_col, lfn) in ((r * 128, "v"), (256 + r * 128, "o")):
                        for step in range(2):
                            for e_ in range(2):
                                prev_ap = ptb[pprev][e_][:, ((qt - 1) % 2) * 256 + 128:((qt - 1) % 2) * 256 + 256]
                                cur_ap = ptb[pcur][e_][:, (qt % 2) * 256:(qt % 2) * 256 + 128]
                                kt_, rhs = ((qt - 1, prev_ap), (qt, cur_ap))[step]
                                rdk = [("pt", pprev, e_), ("pt", pcur, e_), ("V", qt - 1), ("V", qt)]
                                l_ap = V3[:, kt_, kvh * 64:(kvh + 1) * 64] if lfn == "v" else ones2[:, 0:64]
                                P.op("pe", lambda e, od=od, e_=e_, dst_col=dst_col, l_ap=l_ap, rhs=rhs, step=step: e.matmul(
                                    od[64 * e_:64 * e_ + 64, dst_col:dst_col + 128], l_ap, rhs, start=(step == 0), stop=(step == 1),
                                    tile_position=(0, 64 * e_)), rd=rdk + [("c", "ones")], wr=[odk])
                    if r == 1:
                        q0 = (qt - 2) * 128
                        zk = [("szq", qs, bi) for bi in range(2) if q0 < (bi + 1) * 512 and q0 + 256 > bi * 512]
                        P.op("act", lambda e, od=od: e.activation(out=rr[:, :], in_=od[:, 256:512], func=AF.Identity, bias=exps2[:, c:c + 1], scale=1.0),
                             rd=[odk, ("c", "exps")], wr=[("rr",)], group="SC")
                        P.op("dve", lambda e: e.reciprocal(out=rr[:, :], in_=rr[:, :]), rd=[("rr",)], wr=[("rr",)], group="SC")
                        P.op("dve", lambda e, q0=q0: e.tensor_tensor(out=wv[:, :], in0=rr[:, :], in1=szq[qs][:, q0:q0 + 256], op=ALU.mult),
                             rd=[("rr",)] + zk, wr=[("wv",)], group="SC")
                        P.op("dve", lambda e, od=od, q0=q0: e.tensor_tensor(out=G3[:, c, q0:q0 + 256], in0=od[:, 0:256], in1=wv[:, :], op=ALU.mult),
                             rd=[odk, ("wv",)], wr=[("G", c)], group="R1")
            return S, PV

        q_blk(0, 0)
        q_blk(0, 1)
        r_blk(0, 0)
        r_blk(0, 1)
        z_blk(0, 0)
        z_blk(0, 1)
        pend = {"pv4": None}
        for c in range(KC):
            if c % 4 == 0:
                P.op("pool", lambda e, n=c // 4: e.dma_start(out=WO3[:, :, n * 512:(n + 1) * 512],
                                                             in_=w_out1[:, n * 512:(n + 1) * 512].rearrange("(c p) n -> p c n", p=128)),
                     wr=[("WO", c // 4)], dma=("WO", c // 4))
            S_, PV_ = attn_steps(c)
            n_ = c + 1
            nop = lambda: None
            fq0 = fq1 = fr0 = fr1 = fz0 = fz1 = nop
            if n_ < KC:
                fq0 = lambda n_=n_: q_blk(n_, 0)
                fq1 = lambda n_=n_: q_blk(n_, 1)
                fr0 = lambda n_=n_: r_blk(n_, 0)
                fr1 = lambda n_=n_: r_blk(n_, 1)
                fz0 = lambda n_=n_: z_blk(n_, 0)
                fz1 = lambda n_=n_: z_blk(n_, 1)
            S_(0)
            fq0()
            S_(1)
            PV_(0)
            fq1()
            S_(2)
            fr0()
            PV_(1)
            fz0()
            S_(3)
            fr1()
            PV_(2)
            S_(4)
            fz1()
            PV_(3)
            PV_(4)

        if debug_stop in ("l1p1", "l1qz"):
            break
        if s + 1 < S and STAGE_X:
            yusers = P.users("Y")
            for t in range(NT):
                P.op("pool", lambda e, s=s, t=t: e.dma_start(out=Y[:, t * D:(t + 1) * D], in_=xin[s + 1, LB + t * 128:LB + (t + 1) * 128, :]),
                     wr=[("ystage", t)], dma=("ystage", t), extra=yusers)
            staged[0] = True
        phase2(1, G3, TO // 128, 0,
               lambda t, s=s: x1s[s, HL + t * 128:HL + (t + 1) * 128, :],
               lambda t, s=s: (out[s * TO + t * 128:s * TO + (t + 1) * 128, :], ("out", s, t)),
               False)

    fin0 = [o for o in P.q["sp"] if o.dma == ("hout", 0)]
    fin1 = [o for o in P.q["sp"] if o.dma == ("hout", 1)]
    P.op("sp", None, extra=fin0[-1:] + fin1[-1:])

    P.finalize()
    csem = {e: st.enter_context(nc.semaphore(f"cs_{e}")) for e in ("pe", "act", "dve", "pool", "sp")}
    dsem = {k: st.enter_context(nc.semaphore("ds_" + "_".join(str(x) for x in k))) for k in P.dma_cnt}
    with nc.Block() as block:
        @block.tensor
        def _(e):
            P.replay("pe", e, csem, dsem)

        @block.scalar
        def _(e):
            P.replay("act", e, csem, dsem)

        @block.vector
        def _(e):
            P.replay("dve", e, csem, dsem)

        @block.gpsimd
        def _(e):
            P.replay("pool", e, csem, dsem)

        @block.sync
        def _(e):
            P.replay("sp", e, csem, dsem)
    st.close()
    return nc


def host_inputs(x, ln_g, ln_b, a_w_in, a_w_group, a_scale, a_w_out, b_w_k, b_w_v, b_w_qg, b_sinks, b_w_out):
    f32 = np.float32
    x = np.asarray(x, f32)
    shared = {
        "w_in": np.ascontiguousarray(np.asarray(a_w_in, f32)[0]),
        "w_grp": np.ascontiguousarray(np.asarray(a_w_group, f32)[0]),
        "w_out0": np.ascontiguousarray(np.asarray(a_w_out, f32)[0]),
        "w_k": np.ascontiguousarray(np.asarray(b_w_k, f32)),
        "w_v": np.ascontiguousarray(np.asarray(b_w_v, f32)),
        "w_qg": np.ascontiguousarray(np.asarray(b_w_qg, f32)[0]),
        "w_out1": np.ascontiguousarray(np.asarray(b_w_out, f32)[0]),
        "lngb": np.ascontiguousarray(np.concatenate([np.asarray(ln_g, f32), np.asarray(ln_b, f32)], axis=1)),
        "scale_col": np.ascontiguousarray(np.asarray(a_scale, f32)[0].reshape(16, 128).T),
    }
    sk = np.asarray(b_sinks, f32)[0]
    pidx = np.arange(128)[:, None] // 64
    cidx = np.arange(16)[None, :]
    shared["sink_col"] = np.ascontiguousarray(sk[2 * cidx + pidx])
    bf = ml_dtypes.bfloat16
    shared["ident"] = np.eye(128, dtype=f32).astype(bf)
    m = np.arange(128)
    partner = np.where((m % 64) < 32, m + 32, m - 32)
    swp = np.zeros((128, 128), f32)
    swp[partner, m] = 1.0
    shared["swp"] = swp.astype(bf)
    shared["ones2"] = np.full((128, 64), 2.0, f32).astype(bf)
    hsw = np.zeros((128, 128), f32)
    hsw[(m + 64) % 128, m] = 1.0
    shared["hsw"] = hsw.astype(bf)
    triu = np.triu(np.ones((128, 128), f32))
    stril = np.tril(np.ones((128, 128), f32), -1)
    gen = np.concatenate([triu, stril, triu, stril], axis=1)
    d = np.arange(128) % 64
    fidx = (d % 32).astype(f32)
    inv_freq = (np.float32(10000.0) ** (-(2.0 * fidx) / np.float32(64.0))).astype(f32)
    sign = np.where(d < 32, -1.0, 1.0).astype(f32)
    wins = (2, 4, 8, 16)
    in_maps = []
    for core in range(NCORES):
        b, half = core // 2, core % 2
        xin = np.zeros((S, TC, D), f32)
        invc = np.zeros((S, 128, 64), f32)
        cs = np.zeros((S, 128, 2 * TY), f32)
        m2 = np.zeros((S, 128, 640), f32)
        for s in range(S):
            start = half * 2048 + s * TO
            lo = start - (LB + HL)
            src_lo = max(lo, 0)
            xin[s, src_lo - lo:, :] = x[b, src_lo:start + TO, :]
            for g, w in enumerate(wins):
                t = np.arange(16, dtype=f32)
                if start == 0:
                    invc[s, :, g * 16:(g + 1) * 16] = (1.0 / np.minimum(t + 1.0, float(w)))[None, :]
                else:
                    invc[s, :, g * 16:(g + 1) * 16] = 1.0 / w
            pos = (start - HL + np.arange(TY)).astype(f32)
            ang = (pos[None, :] * inv_freq[:, None]).astype(f32).astype(np.float64)
            cs[s, :, 0:TY] = np.cos(ang).astype(f32)
            cs[s, :, TY:] = (np.sin(ang) * sign[:, None]).astype(f32)
            m2[s, :, 0:512] = gen
            m2[s, :, 512:640] = np.zeros_like(stril) if start == 0 else stril
        mp = dict(shared)
        mp["xin"] = xin
        mp["invc"] = invc
        mp["cs"] = cs
        mp["m2"] = m2.astype(bf)
        in_maps.append(mp)
    return in_maps


_NC_CACHE = {}


def kernel(x, ln_g, ln_b, a_w_in, a_w_group, a_scale, a_w_out, b_w_k, b_w_v, b_w_qg, b_sinks, b_w_out):
    in_maps = host_inputs(x, ln_g, ln_b, a_w_in, a_w_group, a_scale, a_w_out, b_w_k, b_w_v, b_w_qg, b_sinks, b_w_out)
    if "nc" not in _NC_CACHE:
        _NC_CACHE["nc"] = build_program()
    nc = _NC_CACHE["nc"]
    res = run_bass_kernel_spmd(nc, in_maps, core_ids=list(range(NCORES)))
    outp = np.zeros((4, 4096, D), np.float32)
    for core in range(NCORES):
        b, half = core // 2, core % 2
        outp[b, half * 2048:(half + 1) * 2048, :] = np.asarray(res.results[core]["out"], np.float32).reshape(S * TO, D)
    return outp
```

```python
import numpy as np
from contextlib import ExitStack
import ml_dtypes
import concourse.bass as bass
import concourse.mybir as mybir
from concourse.bass_utils import run_bass_kernel_spmd

F32 = mybir.dt.float32
BF = mybir.dt.bfloat16
AF = mybir.ActivationFunctionType
ALU = mybir.AluOpType

D = 2048
KC = 16
S = 2
TO = 1024
HL = 128
LB = 16
TC = LB + HL + TO
TY = HL + TO
NT = TY // 128
ALPHA = float((2 * 2) ** 0.25)
EPS = 1e-5
NCORES = 8
SAME_ENG_SYNC_MAX = 256
SAME_ENG_ALL = True
STAGE_X = True


GROUP_KEYS = {
    "SC": {"xb", "u", "sA", "sB", "pooled", "sz", "tmp16", "gb", "xt", "h", "xb2", "cs", "KT", "kb", "t1", "t2", "th", "szq", "qr", "pt", "rr", "wv"},
    "R1": {"X", "G", "V"},
}


class Op:
    __slots__ = ("eng", "fn", "deps", "sig", "sigval", "dma", "dmaval", "small", "idx")


class Prog:
    ENGS = ("pe", "act", "dve", "pool", "sp")

    def __init__(self):
        self.q = {e: [] for e in self.ENGS}
        self.last_w = {}
        self.rd_eng = {}
        self.rd_dma = {}
        self.dma_cnt = {}
        self.group_of = {}
        self.group_extra = {}
        self.n = 0

    def barrier(self, group):
        ops = list(self.group_extra.get(group, []))
        for k, g in list(self.group_of.items()):
            if g != group:
                continue
            w = self.last_w.pop(k, None)
            if w is not None:
                ops.append(w)
            ops.extend(self.rd_eng.pop(k, {}).values())
            ops.extend(self.rd_dma.pop(k, []))
            del self.group_of[k]
        best = {}
        out = []
        for o in ops:
            if o.dma:
                out.append(o)
            else:
                b = best.get(o.eng)
                if b is None or o.idx > b.idx:
                    best[o.eng] = o
        out.extend(best.values())
        self.group_extra[group] = out

    def op(self, eng, fn, rd=(), wr=(), dma=None, small=False, group=None, extra=()):
        o = Op()
        o.eng = eng
        o.fn = fn
        o.dma = dma
        o.sig = False
        o.sigval = 0
        o.dmaval = 0
        o.small = small
        o.idx = self.n
        self.n += 1
        deps = {}
        for d in extra:
            deps[id(d)] = d
        if group is not None:
            for d in self.group_extra.get(group, ()):
                deps[id(d)] = d
            for k in list(rd) + list(wr):
                if k[0] in GROUP_KEYS[group]:
                    self.group_of[k] = group
        for k in rd:
            w = self.last_w.get(k)
            if w is not None:
                deps[id(w)] = w
            if k[0] == "ps":
                for r in self.rd_eng.get(k, {}).values():
                    if r.eng != eng:
                        deps[id(r)] = r
        for k in wr:
            w = self.last_w.get(k)
            if w is not None:
                deps[id(w)] = w
            for r in self.rd_eng.get(k, {}).values():
                deps[id(r)] = r
            for r in self.rd_dma.get(k, ()):
                deps[id(r)] = r
        o.deps = [d for d in deps.values() if d is not o]
        for d in o.deps:
            d.sig = True
        for k in rd:
            if dma:
                self.rd_dma.setdefault(k, []).append(o)
            else:
                self.rd_eng.setdefault(k, {})[eng] = o
        for k in wr:
            self.last_w[k] = o
            self.rd_eng[k] = {}
            self.rd_dma[k] = []
        if dma:
            c = self.dma_cnt.get(dma, 0) + 16
            self.dma_cnt[dma] = c
            o.dmaval = c
        self.q[eng].append(o)
        return o

    def users(self, prefix):
        ops = []
        for k, w in self.last_w.items():
            if k[0] == prefix:
                ops.append(w)
        for k, d in self.rd_eng.items():
            if k[0] == prefix:
                ops.extend(d.values())
        for k, l in self.rd_dma.items():
            if k[0] == prefix:
                ops.extend(l)
        return ops

    def finalize(self):
        for e in self.ENGS:
            c = 0
            for o in self.q[e]:
                if o.sig and not o.dma:
                    c += 1
                    o.sigval = c

    def replay(self, ename, eng, csem, dsem):
        waited = {}
        for o in self.q[ename]:
            for d in o.deps:
                if d.dma:
                    key = ("d", d.dma)
                    sem = dsem[d.dma]
                    val = d.dmaval
                else:
                    if d.eng == ename:
                        if ename == "pe" or not (d.small or SAME_ENG_ALL):
                            continue
                    key = ("c", d.eng)
                    sem = csem[d.eng]
                    val = d.sigval
                if waited.get(key, 0) >= val:
                    continue
                eng.wait_ge(sem, val)
                waited[key] = val
            if o.fn is None:
                continue
            ins = o.fn(eng)
            if o.dma:
                ins.then_inc(dsem[o.dma], 16)
            elif o.sig:
                ins.then_inc(csem[ename], 1)


def build_program(debug_stop=None):
    nc = bass.Bass("TRN2", target_bir_lowering=False)
    P = Prog()

    def dram(name, shape, dt, kind="ExternalInput"):
        return nc.dram_tensor(name, list(shape), dt, kind=kind).ap()

    xin = dram("xin", [S, TC, D], F32)
    w_in = dram("w_in", [D, 2 * D], F32)
    w_grp = dram("w_grp", [4, 512, 512], F32)
    w_out0 = dram("w_out0", [D, D], F32)
    w_k = dram("w_k", [D, 256], F32)
    w_v = dram("w_v", [D, 256], F32)
    w_qg = dram("w_qg", [D, 2 * D], F32)
    w_out1 = dram("w_out1", [D, D], F32)
    lngb = dram("lngb", [2, 2 * D], F32)
    scale_col = dram("scale_col", [128, 16], F32)
    sink_col = dram("sink_col", [128, 16], F32)
    invc_in = dram("invc", [S, 128, 64], F32)
    cs_in = dram("cs", [S, 128, 2 * TY], F32)
    m2_in = dram("m2", [S, 128, 640], BF)
    hsw_in = dram("hsw", [128, 128], BF)
    ident_in = dram("ident", [128, 128], BF)
    swp_in = dram("swp", [128, 128], BF)
    ones_in = dram("ones2", [128, 64], BF)
    out = dram("out", [S * TO, D], F32, kind="ExternalOutput")
    x1s = dram("x1s", [S, TY, D], F32, kind="Internal")

    st = ExitStack()

    def sb(name, shape, dt):
        return st.enter_context(nc.sbuf_tensor(name, list(shape), dt))

    XG = sb("XG", [128, KC * TC], BF)
    Y = sb("Y", [128, KC * TY], BF)
    WO = sb("WO", [128, KC * D], BF)
    WS = [sb(f"WS{i}", [128, KC * 256], BF) for i in range(3)]
    SCB = 45056
    SC = sb("SC", [128, SCB // 2], BF)
    ident = sb("identb", [128, 128], BF)
    swp = sb("swpb", [128, 128], BF)
    ones2 = sb("ones2b", [128, 64], BF)
    m2 = sb("m2b", [128, 640], BF)
    hsw = sb("hswb", [128, 128], BF)
    scol = sb("scol", [128, 16], F32)
    sinkc = sb("sinkc", [128, 16], F32)
    exps2 = sb("exps2", [128, 16], F32)
    invc = sb("invcb", [128, 64], F32)
    stt_ = sb("stats", [128, 32], F32)
    epsc = sb("epsc", [128, 1], F32)
    PS = [st.enter_context(nc.psum_tensor(f"ps{i}", [128, 512], F32)) for i in range(8)]

    X3 = XG[:, :].rearrange("p (c t) -> p c t", c=KC)
    Y3 = Y[:, :].rearrange("p (c t) -> p c t", c=KC)
    G3 = XG[:, 0:KC * TO].rearrange("p (c t) -> p c t", c=KC)
    V3 = XG[:, KC * TO:KC * TO + NT * 256].rearrange("p (t n) -> p t n", t=NT)
    WO3 = WO[:, :].rearrange("p (c n) -> p c n", c=KC)
    WS3 = [w[:, :].rearrange("p (c n) -> p c n", c=KC) for w in WS]

    sc_off = [0]

    def carve(nbytes):
        o = sc_off[0]
        assert o % 4 == 0
        sc_off[0] = o + nbytes
        assert sc_off[0] <= SCB, sc_off[0]
        return SC[:, o // 2:(o + nbytes) // 2]

    def carve_f32(n):
        return carve(n * 4).bitcast(F32)

    def carve_bf(n):
        return carve(n * 2)

    PSb = [PS[6][:, :].bitcast(BF), PS[7][:, :].bitcast(BF)]

    jobs = []
    job_slot = {}
    issued = [0]

    def slab_job(src_ap, ncols):
        def fn(slot):
            dst = WS3[slot][:, :, 0:ncols]
            src = src_ap.rearrange("(c p) n -> p c n", p=128)
            P.op("pool", lambda e: e.dma_start(out=dst, in_=src), wr=[("ws", slot)], dma=("ws", slot))
        jobs.append(fn)
        return len(jobs) - 1

    def grp_job(g):
        def fn(slot):
            d = WS[slot][:, 0:2048].rearrange("p (c n) -> p c n", c=4)
            P.op("pool", lambda e: e.dma_start(out=d, in_=w_grp[g].rearrange("(c p) n -> p c n", p=128)),
                 wr=[("ws", slot)], dma=("ws", slot))
        jobs.append(fn)
        return len(jobs) - 1

    def k_job(half):
        def fn(slot):
            for jj in range(2):
                j = half * 2 + jj
                for r in range(2):
                    P.op("pool", lambda e, jj=jj, j=j, r=r: e.dma_start(
                        out=WS3[slot][:, :, jj * 128 + r * 64:jj * 128 + (r + 1) * 64],
                        in_=w_k[:, j * 64:(j + 1) * 64].rearrange("(c p) n -> p c n", p=128)), wr=[("ws", slot)], dma=("ws", slot))
        jobs.append(fn)
        return len(jobs) - 1

    def ensure(k):
        while issued[0] <= min(k, len(jobs) - 1):
            i = issued[0]
            slot = i % 3
            job_slot[i] = slot
            jobs[i](slot)
            issued[0] += 1

    def use(k, ahead=2):
        ensure(k + ahead)
        return job_slot[k]

    JL0 = []
    JL1 = []
    for s_ in range(S):
        d0 = {}
        for g in range(4):
            for j in range(2):
                d0[(g, "u", j)] = slab_job(w_in[:, (2 * g + j) * 256:(2 * g + j + 1) * 256], 256)
            for j in range(2):
                d0[(g, "z", j)] = slab_job(w_in[:, D + (2 * g + j) * 256:D + (2 * g + j + 1) * 256], 256)
            d0[(g, "w")] = grp_job(g)
        JL0.append(d0)
        d1 = {}
        d1["K"] = slab_job(w_k[:, :], 256)
        d1["V"] = slab_job(w_v[:, :], 256)
        for p_ in range(8):
            d1[("q", p_)] = slab_job(w_qg[:, p_ * 256:(p_ + 1) * 256], 256)
            d1[("z", p_)] = slab_job(w_qg[:, D + p_ * 256:D + (p_ + 1) * 256], 256)
        JL1.append(d1)

    def mm_group(out_ap, pairs, rd, wr, **kw):
        n = len(pairs)
        last = None
        for j, (l, r) in enumerate(pairs):
            last = P.op("pe", lambda e, o=out_ap, l=l, r=r, a=(j == 0), b=(j == n - 1): e.matmul(o, l, r, start=a, stop=b, **kw),
                        rd=rd if j == n - 1 else (), wr=wr if j == n - 1 else ())
        return last

    def mm_group2(out_ap, pairs, rd, wr, **kw):
        n = len(pairs)
        for j, (l, r) in enumerate(pairs):
            P.op("pe", lambda e, o=out_ap, l=l, r=r, a=(j == 0), b=(j == n - 1): e.matmul(o, l, r, start=a, stop=b, **kw),
                 rd=rd if (j == 0 or j == n - 1) else (), wr=wr if (j == 0 or j == n - 1) else ())

    def cload(dst, src, key):
        P.op("sp", lambda e: e.dma_start(out=dst, in_=src), wr=[key], dma=key)

    cload(ident[:, :], ident_in, ("c", "ident"))
    cload(swp[:, :], swp_in, ("c", "swp"))
    cload(hsw[:, :], hsw_in, ("c", "hsw"))
    cload(ones2[:, :], ones_in, ("c", "ones"))
    cload(scol[:, :], scale_col, ("c", "scol"))
    cload(sinkc[:, :], sink_col, ("c", "sink"))
    P.op("dve", lambda e: e.memset(epsc[:, :], EPS), wr=[("c", "eps")], small=True)
    P.op("act", lambda e: e.activation(out=exps2[:, :], in_=sinkc[:, :], func=AF.Exp), rd=[("c", "sink")], wr=[("c", "exps")], small=True)
    P.op("dve", lambda e: e.tensor_scalar(out=exps2[:, :], in0=exps2[:, :], scalar1=2.0, scalar2=None, op0=ALU.mult),
         rd=[("c", "exps")], wr=[("c", "exps")], small=True)

    def xkeys(a, b):
        ks = []
        if a < LB:
            ks.append(("X", "lb"))
        for t in range(NT):
            lo, hi = LB + t * 128, LB + (t + 1) * 128
            if a < hi and b > lo:
                ks.append(("X", t))
        return ks

    def ykeys(a, b):
        return [("Y", t) for t in range(NT) if a < (t + 1) * 128 and b > t * 128]

    acc_ring = {"i": 0}

    def next_acc(banks):
        b = banks[acc_ring["i"] % len(banks)]
        acc_ring["i"] += 1
        return b

    staged = [False]
    lbst = [None]
    for s in range(S):
        P.barrier("R1")
        P.barrier("SC")
        sc_off[0] = 0
        NXB = 6
        xb = [carve_bf(D) for _ in range(NXB)]
        P.op("sp", lambda e, s=s: e.dma_start(out=m2[:, :], in_=m2_in[s]), wr=[("c", "m2")], dma=("c", "m2"))
        P.op("sp", lambda e, s=s: e.dma_start(out=invc[:, :], in_=invc_in[s]), wr=[("c", "invc")], dma=("c", "invc"))

        def transposes(src, rows, tkey, dst3, col0, ncol, dst_key, grp, bset=1, extra=()):
            for half in range(2):
                bank = PS[4 + 2 * bset + half]
                bkey = ("ps", id(bank))
                pb = bank[:, :].bitcast(BF)
                for cc in range(8):
                    c = half * 8 + cc
                    P.op("pe", lambda e, c=c, cc=cc, pb=pb: e.transpose(
                        pb[:, cc * ncol:(cc + 1) * ncol], src[0:rows, c * 128:(c + 1) * 128], ident[0:rows, 0:rows]),
                        rd=[tkey, ("c", "ident")] if cc in (0, 7) else (), wr=[bkey] if cc in (0, 7) else (),
                        extra=extra if cc == 0 else ())
                eng = "act" if half == 0 else "dve"
                src_ps = pb[:, 0:8 * ncol].rearrange("p (c t) -> p c t", c=8)
                dst = dst3[:, half * 8:(half + 1) * 8, col0:col0 + ncol]
                if eng == "act":
                    P.op("act", lambda e, d=dst, s_=src_ps: e.activation(out=d, in_=s_, func=AF.Copy), rd=[bkey], wr=[dst_key], group=grp)
                else:
                    P.op("dve", lambda e, d=dst, s_=src_ps: e.tensor_copy(out=d, in_=s_), rd=[bkey], wr=[dst_key], group=grp)

        if staged[0]:
            ensure(JL0[s][(0, "u", 0)] + 2)
        if staged[0]:
            transposes(lbst[0], LB, ("lbst",), X3, 0, LB, ("X", "lb"), "R1")
        else:
            P.op("pool", lambda e, s=s: e.dma_start(out=xb[0][0:LB, :], in_=xin[s, 0:LB, :]), wr=[("xb", 0)], dma=("xb", 0), group="SC")
            transposes(xb[0], LB, ("xb", 0), X3, 0, LB, ("X", "lb"), "R1")
        stage_users = []
        for t in range(NT):
            if staged[0]:
                ys = Y[:, t * D:(t + 1) * D]
                transposes(ys, 128, ("ystage", t), X3, LB + t * 128, 128, ("X", t), "R1", bset=t % 2)
                stage_users = [P.q["pe"][-1]]
            else:
                sl = (t + 1) % NXB
                P.op("pool", lambda e, s=s, t=t, sl=sl: e.dma_start(out=xb[sl][:, :], in_=xin[s, LB + t * 128:LB + (t + 1) * 128, :]),
                     wr=[("xb", sl)], dma=("xb", sl), group="SC")
                if t == 3:
                    ensure(JL0[s][(0, "u", 0)])
                if t == NT - 1:
                    ensure(JL0[s][(0, "u", 0)] + 2)
                transposes(xb[sl], 128, ("xb", sl), X3, LB + t * 128, 128, ("X", t), "R1", bset=t % 2)
        staged[0] = False

        if debug_stop == "pro":
            break
        P.barrier("SC")
        sc_off[0] = 0
        ubuf = [carve_f32(TC), carve_f32(TC)]
        sA = carve_f32(TC)
        sBf = carve_f32(TC)
        pooled = carve_bf(4 * TY).rearrange("p (c t) -> p c t", c=4)
        szb = carve_bf(4 * TY).rearrange("p (c t) -> p c t", c=4)
        tmp16 = carve_f32(16)

        ublocks = [(0, LB), (LB, LB + 512), (LB + 512, LB + 1024), (LB + 1024, TC)]
        yblocks = [(0, 512), (512, 1024), (1024, TY)]
        ACC0 = [PS[0], PS[1], PS[2], PS[3]]
        GRP0 = [PS[4], PS[5]]
        WIN = (2, 4, 8, 16)
        for g in range(4):
            P.op("pool", lambda e, n=g: e.dma_start(out=WO3[:, :, n * 512:(n + 1) * 512],
                                                    in_=w_out0[:, n * 512:(n + 1) * 512].rearrange("(c p) n -> p c n", p=128)),
                 wr=[("WO", g)], dma=("WO", g))
            for m in range(4):
                c = 4 * g + m
                wsl = use(JL0[s][(g, "u", m // 2)])
                us = c % 2
                for (a, b) in ublocks:
                    bank = next_acc(ACC0)
                    bk = ("ps", id(bank))
                    mm_group2(bank[:, 0:b - a], [(WS3[wsl][:, kc, (m % 2) * 128:(m % 2 + 1) * 128], X3[:, kc, a:b]) for kc in range(KC)],
                              rd=[("ws", wsl)] + xkeys(a, b), wr=[bk])
                    P.op("act", lambda e, bank=bank, a=a, b=b, us=us: e.activation(out=ubuf[us][:, a:b], in_=bank[:, 0:b - a], func=AF.Copy),
                         rd=[bk], wr=[("u", us)], group="SC", small=(b - a) <= SAME_ENG_SYNC_MAX)
                w = WIN[g]
                u = ubuf[us]
                uk = ("u", us)
                P.op("dve", lambda e, u=u: e.tensor_tensor(out=sA[:, 1:TC], in0=u[:, 1:TC], in1=u[:, 0:TC - 1], op=ALU.add),
                     rd=[uk], wr=[("sA",)], group="SC")
                fin, fk = sA, ("sA",)
                if w >= 4:
                    P.op("dve", lambda e: e.tensor_tensor(out=sBf[:, 3:TC], in0=sA[:, 3:TC], in1=sA[:, 1:TC - 2], op=ALU.add),
                         rd=[("sA",)], wr=[("sB",)], group="SC")
                    fin, fk = sBf, ("sB",)
                if w >= 8:
                    P.op("dve", lambda e: e.tensor_tensor(out=sA[:, 7:TC], in0=sBf[:, 7:TC], in1=sBf[:, 3:TC - 4], op=ALU.add),
                         rd=[("sB",)], wr=[("sA",)], group="SC")
                    fin, fk = sA, ("sA",)
                if w >= 16:
                    P.op("dve", lambda e: e.tensor_tensor(out=sBf[:, 15:TC], in0=sA[:, 15:TC], in1=sA[:, 7:TC - 8], op=ALU.add),
                         rd=[("sA",)], wr=[("sB",)], group="SC")
                    fin, fk = sBf, ("sB",)
                P.op("dve", lambda e, fin=fin, u=u, m=m, w=w: e.scalar_tensor_tensor(
                    out=pooled[:, m, :], in0=fin[:, LB:TC], scalar=1.0 / w, in1=u[:, LB:TC], op0=ALU.mult, op1=ALU.subtract),
                    rd=[fk, uk], wr=[("pooled", m)], group="SC")
                o0 = LB + HL
                P.op("dve", lambda e, fin=fin, g=g: e.tensor_tensor(out=tmp16[:, :], in0=fin[:, o0:o0 + 16], in1=invc[:, g * 16:(g + 1) * 16], op=ALU.mult),
                     rd=[fk, ("c", "invc")], wr=[("tmp16",)], group="SC", small=True)
                P.op("dve", lambda e, u=u, m=m: e.tensor_tensor(out=pooled[:, m, HL:HL + 16], in0=tmp16[:, :], in1=u[:, o0:o0 + 16], op=ALU.subtract),
                     rd=[("tmp16",), uk], wr=[("pooled", m)], group="SC", small=True)
            for m in range(4):
                wsl = use(JL0[s][(g, "z", m // 2)])
                for (a, b) in ublocks[1:]:
                    bank = next_acc(ACC0)
                    bk = ("ps", id(bank))
                    mm_group2(bank[:, 0:b - a], [(WS3[wsl][:, kc, (m % 2) * 128:(m % 2 + 1) * 128], X3[:, kc, a:b]) for kc in range(KC)],
                              rd=[("ws", wsl)] + xkeys(a, b), wr=[bk])
                    P.op("act", lambda e, bank=bank, a=a, b=b, m=m: e.activation(out=szb[:, m, a - LB:b - LB], in_=bank[:, 0:b - a], func=AF.Silu),
                         rd=[bk], wr=[("sz", m)], group="SC")
            gi = use(JL0[s][(g, "w")])
            wg3 = WS[gi][:, 0:2048].rearrange("p (c n) -> p c n", c=4)
            for m in range(4):
                c = 4 * g + m
                for bi, (a, b) in enumerate(yblocks):
                    bank = GRP0[(m * 3 + bi) % 2]
                    bk = ("ps", id(bank))
                    mm_group2(bank[:, 0:b - a], [(wg3[:, kc, m * 128:(m + 1) * 128], pooled[:, kc, a:b]) for kc in range(4)],
                              rd=[("ws", gi)] + [("pooled", k) for k in range(4)], wr=[bk])
                    P.op("dve", lambda e, bank=bank, a=a, b=b, m=m, c=c: e.scalar_tensor_tensor(
                        out=Y3[:, c, a:b], in0=bank[:, 0:b - a], scalar=scol[:, c:c + 1], in1=szb[:, m, a:b], op0=ALU.mult, op1=ALU.mult),
                        rd=[bk, ("sz", m), ("c", "scol")], wr=ykeys(a, b), extra=stage_users)

        if debug_stop == "l0p1":
            break
        def phase2(layer, act3, ntiles, tile_col0, res_src, dst_fn, do_transpose):
            P.barrier("SC")
            sc_off[0] = 0
            gb = carve_f32(2 * D)
            xt1 = carve_f32(D)
            xt = [xt1, xt1]
            hh = [carve_f32(D), carve_f32(D)]
            xb2 = carve_bf(D)
            if layer == 1 and s + 1 < S and STAGE_X:
                lbst[0] = xb2
                P.op("pool", lambda e, s=s: e.dma_start(out=xb2[0:LB, :], in_=xin[s + 1, 0:LB, :]), wr=[("lbst",)], dma=("lbst",), extra=P.group_extra.get("SC", []))
            P.op("sp", lambda e: e.dma_start(out=gb[:, :], in_=lngb[layer:layer + 1, :].partition_broadcast(128)),
                 wr=[("gb",)], dma=("gb",), group="SC")
            OUTB = [PS[0], PS[1], PS[2], PS[3]]

            casts = {}

            def tail(t, extra=()):
                transposes(xb2, 128, ("xb2",), Y3, t * 128, 128, ("Y", t), None, extra=extra)

            for t in range(ntiles):
                xs = 0
                h = hh[t % 2]
                hs = t % 2
                if t == 0:
                    P.op("sp", lambda e, t=t, xs=xs: e.dma_start(out=xt[xs][:, :], in_=res_src(t)), rd=[("x1s", s, t)] if layer == 1 else (),
                         wr=[("xt", xs)], dma=("xt", xs), group="SC")
                c0 = tile_col0 + t * 128
                akeys = ykeys(c0, c0 + 128) if layer == 0 else [("G", kc) for kc in range(KC)]
                for n in range(4):
                    bk = ("ps", id(OUTB[n]))
                    mm_group2(OUTB[n][:, :], [(act3[:, kc, c0:c0 + 128], WO3[:, kc, n * 512:(n + 1) * 512]) for kc in range(KC)],
                              rd=akeys + [("WO", n)], wr=[bk])
                    P.op("dve", lambda e, n=n, xs=xs, h=h: e.scalar_tensor_tensor(
                        out=h[:, n * 512:(n + 1) * 512], in0=xt[xs][:, n * 512:(n + 1) * 512], scalar=ALPHA, in1=OUTB[n][:, :],
                        op0=ALU.mult, op1=ALU.add), rd=[bk, ("xt", xs)], wr=[("h", hs, n)], group="SC")
                    P.op("dve", lambda e, n=n, h=h: e.bn_stats(out=stt_[:, n * 6:(n + 1) * 6], in_=h[:, n * 512:(n + 1) * 512]),
                         rd=[("h", hs, n)], wr=[("st", n)], small=True)
                if t + 1 < ntiles:
                    P.op("sp", lambda e, t=t, xs=xs: e.dma_start(out=xt[xs][:, :], in_=res_src(t + 1)), rd=[("x1s", s, t + 1)] if layer == 1 else (),
                         wr=[("xt", xs)], dma=("xt", xs), group="SC")
                if do_transpose and t > 0:
                    tail(t - 1)
                hk = [("h", hs, n) for n in range(4)]
                P.op("dve", lambda e: e.bn_aggr(out=stt_[:, 24:26], in_=stt_[:, 0:24]), rd=[("st", n) for n in range(4)], wr=[("mv",)], small=True)
                P.op("act", lambda e: e.activation(out=stt_[:, 28:29], in_=stt_[:, 25:26], func=AF.Sqrt, bias=epsc[:, 0:1], scale=1.0),
                     rd=[("mv",), ("c", "eps")], wr=[("sd",)], small=True)
                P.op("dve", lambda e: e.reciprocal(out=stt_[:, 26:27], in_=stt_[:, 28:29]), rd=[("sd",)], wr=[("rs",)], small=True)
                P.op("dve", lambda e: e.scalar_tensor_tensor(out=stt_[:, 27:28], in0=stt_[:, 24:25], scalar=-1.0, in1=stt_[:, 26:27],
                                                             op0=ALU.mult, op1=ALU.mult), rd=[("mv",), ("rs",)], wr=[("nmr",)], small=True)
                P.op("act", lambda e, h=h: e.activation(out=h[:, :], in_=h[:, :], func=AF.Identity, bias=stt_[:, 27:28], scale=stt_[:, 26:27]),
                     rd=hk + [("rs",), ("nmr",)], wr=hk, group="SC")
                hkA = [("h", hs, 0), ("h", hs, 1)]
                hkB = [("h", hs, 2), ("h", hs, 3)]
                H2 = D // 2
                P.op("pool", lambda e, h=h: e.tensor_tensor(out=h[:, 0:H2], in0=h[:, 0:H2], in1=gb[:, 0:H2], op=ALU.mult), rd=hkA + [("gb",)], wr=hkA, group="SC")
                P.op("dve", lambda e, h=h: e.tensor_tensor(out=h[:, H2:D], in0=h[:, H2:D], in1=gb[:, H2:D], op=ALU.mult), rd=hkB + [("gb",)], wr=hkB, group="SC")
                P.op("dve", lambda e, h=h: e.tensor_tensor(out=h[:, H2:D], in0=h[:, H2:D], in1=gb[:, D + H2:2 * D], op=ALU.add), rd=hkB + [("gb",)], wr=hkB, group="SC")
                P.op("dve", lambda e, h=h: e.tensor_tensor(out=h[:, 0:H2], in0=h[:, 0:H2], in1=gb[:, D:D + H2], op=ALU.add), rd=hkA + [("gb",)], wr=hkA, group="SC")
                dst, dkey = dst_fn(t)
                if dst is not None:
                    P.op("sp", lambda e, dst=dst, h=h: e.dma_start(out=dst, in_=h[:, :]), rd=hk, wr=[dkey], dma=("hout", hs), group="SC")
                if do_transpose:
                    casts[t] = P.op("act", lambda e, h=h: e.activation(out=xb2[:, :], in_=h[:, :], func=AF.Copy), rd=hk, wr=[("xb2",)], group="SC")
            if do_transpose:
                return lambda: tail(ntiles - 1, extra=[casts[ntiles - 1]])
            return None

        last_tail = phase2(0, Y3, NT, 0,
                           lambda t, s=s: xin[s, LB + t * 128:LB + (t + 1) * 128, :],
                           lambda t, s=s: (x1s[s, t * 128:(t + 1) * 128, :], ("x1s", s, t)),
                           True)
        if debug_stop == "l0p2":
            last_tail()

        if debug_stop == "l0p2":
            break
        P.barrier("SC")
        P.barrier("R1")
        sc_off[0] = 0
        cs = carve_f32(2 * TY)
        KT2 = carve_bf(4 * TY).rearrange("p (c t) -> p c t", c=4)
        kb = carve_bf(TO)
        qr = [carve_bf(TO), carve_bf(TO)]
        t1 = carve_f32(TO)
        t2 = carve_f32(512)
        th = carve_bf(TO)
        szq = [carve_bf(TO), carve_bf(TO)]
        ptb = [[carve_bf(512), carve_bf(512)] for _ in range(3)]
        rr = carve_f32(256)
        wv = carve_f32(256)
        P.op("sp", lambda e, s=s: e.dma_start(out=cs[:, :], in_=cs_in[s]), wr=[("cs",)], dma=("cs",), group="SC")
        ACC1 = [PS[0], PS[1], PS[4], PS[5]]
        cosT = cs[:, 0:TY]
        sinT = cs[:, TY:2 * TY]

        def rope_block(bank, bk, col_a, col_b, dst_ap, dst_key, n, grp):
            P.op("act", lambda e: e.activation(out=kb[:, 0:n], in_=bank[:, 0:n], func=AF.Copy), rd=[bk], wr=[("kb", 0), ("kb", 1)], group="SC")
            P.op("dve", lambda e: e.tensor_tensor(out=t1[:, 0:n], in0=bank[:, 0:n], in1=cosT[:, col_a:col_b], op=ALU.mult),
                 rd=[bk, ("cs",)], wr=[("t1", 0), ("t1", 1)], group="SC")
            rb = next_acc(ACC1)
            rk = ("ps", id(rb))
            P.op("pe", lambda e: e.matmul(rb[:, 0:n], swp[:, :], kb[:, 0:n], start=True, stop=True), rd=[("kb", 0), ("kb", 1), ("c", "swp")], wr=[rk])
            P.op("dve", lambda e: e.tensor_tensor(out=t2[:, 0:n], in0=rb[:, 0:n], in1=sinT[:, col_a:col_b], op=ALU.mult),
                 rd=[rk, ("cs",)], wr=[("t2",)], group="SC")
            P.op("pool", lambda e: e.tensor_tensor(out=dst_ap, in0=t1[:, 0:n], in1=t2[:, 0:n], op=ALU.add),
                 rd=[("t1", 0), ("t1", 1), ("t2",)], wr=[dst_key], group=grp)

        ki = use(JL1[s]["K"])
        vi = use(JL1[s]["V"], ahead=1)
        kunits = [(jc, a, b) for jc in range(2) for (a, b) in yblocks]
        kstate = {}

        def kA(u):
            jc, a, b = kunits[u]
            n = b - a
            bank = next_acc(ACC1)
            bk = ("ps", id(bank))
            mm_group2(bank[:, 0:n], [(WS3[ki][:, kc, jc * 128:(jc + 1) * 128], Y3[:, kc, a:b]) for kc in range(KC)],
                      rd=[("ws", ki)] + ykeys(a, b), wr=[bk])
            P.op("act", lambda e: e.activation(out=kb[:, 0:n], in_=bank[:, 0:n], func=AF.Copy), rd=[bk], wr=[("kb", 0), ("kb", 1)], group="SC")
            P.op("dve", lambda e: e.tensor_tensor(out=t1[:, 0:n], in0=bank[:, 0:n], in1=cosT[:, a:b], op=ALU.mult),
                 rd=[bk, ("cs",)], wr=[("t1", 0), ("t1", 1)], group="SC")

        def kB(u):
            jc, a, b = kunits[u]
            n = b - a
            rb = next_acc(ACC1)
            rk = ("ps", id(rb))
            P.op("pe", lambda e: e.matmul(rb[:, 0:n], swp[:, :], kb[:, 0:n], start=True, stop=True), rd=[("kb", 0), ("kb", 1), ("c", "swp")], wr=[rk])
            P.op("dve", lambda e: e.tensor_tensor(out=t2[:, 0:n], in0=rb[:, 0:n], in1=sinT[:, a:b], op=ALU.mult),
                 rd=[rk, ("cs",)], wr=[("t2",)], group="SC")
            P.op("pool", lambda e: e.tensor_tensor(out=KT2[:, 2 * jc, a:b], in0=t1[:, 0:n], in1=t2[:, 0:n], op=ALU.add),
                 rd=[("t1", 0), ("t1", 1), ("t2",)], wr=[("KT", 2 * jc, a)], group="SC")

        def kS(u):
            jc, a, b = kunits[u]
            n = b - a
            sbk = next_acc(ACC1)
            sk_ = ("ps", id(sbk))
            P.op("pe", lambda e: e.matmul(sbk[:, 0:n], hsw[:, :], KT2[:, 2 * jc, a:b], start=True, stop=True),
                 rd=[("KT", 2 * jc, a), ("c", "hsw")], wr=[sk_])
            P.op("act", lambda e: e.activation(out=KT2[:, 2 * jc + 1, a:b], in_=sbk[:, 0:n], func=AF.Copy),
                 rd=[sk_], wr=[("KT", 2 * jc + 1, a)], group="SC")

        def vT(t):
            bank = next_acc(ACC1)
            bk = ("ps", id(bank))
            mm_group2(bank[:, 0:256], [(Y3[:, kc, t * 128:(t + 1) * 128], WS3[vi][:, kc, 0:256]) for kc in range(KC)],
                      rd=[("ws", vi), ("Y", t)], wr=[bk])
            P.op("act", lambda e: e.activation(out=V3[:, t, :], in_=bank[:, 0:256], func=AF.Copy), rd=[bk], wr=[("V", t)], group="R1")

        def do_tail(i_):
            last_tail()
            P.group_extra.setdefault("SC", []).append(P.q["pe"][-1])

        seq = [("A", 0), ("V", 0), ("V", 1), ("T", 0), ("B", 0), ("A", 1), ("V", 2), ("S", 0), ("B", 1), ("A", 2), ("V", 3), ("S", 1), ("B", 2),
               ("A", 3), ("V", 4), ("S", 2), ("B", 3), ("A", 4), ("V", 5), ("S", 3), ("B", 4), ("A", 5), ("V", 6), ("S", 4), ("B", 5), ("V", 7),
               ("S", 5), ("V", 8)]
        for kind, idx in seq:
            {"A": kA, "B": kB, "S": kS, "V": vT, "T": do_tail}[kind](idx)
        if debug_stop == "l1v":
            break
        SCR = [(PS[2], PS[3]), (PS[2], PS[3])]
        ODB = [PS[6], PS[7]]
        qblocks = [(HL, HL + 512), (HL + 512, HL + 1024)]
        pt_ring = {"i": 0}
        od_ring = {"i": 0}
        slabs_of = {}

        def slabs(c):
            p_ = c // 2
            if p_ not in slabs_of:
                sq = use(JL1[s][("q", p_)])
                slabs_of[p_] = (sq, job_slot[JL1[s][("z", p_)]])
            return slabs_of[p_]

        def q_blk(c, bi):
            sq, _ = slabs(c)
            co = (c % 2) * 128
            a, b = qblocks[bi]
            bank = next_acc(ACC1)
            bk = ("ps", id(bank))
            mm_group2(bank[:, 0:512], [(WS3[sq][:, kc, co:co + 128], Y3[:, kc, a:b]) for kc in range(KC)],
                      rd=[("ws", sq)] + ykeys(a, b), wr=[bk])
            P.op("act", lambda e: e.activation(out=kb[:, bi * 512:(bi + 1) * 512], in_=bank[:, :], func=AF.Copy),
                 rd=[bk], wr=[("kb", bi)], group="SC")
            P.op("dve", lambda e: e.tensor_tensor(out=t1[:, bi * 512:(bi + 1) * 512], in0=bank[:, :], in1=cosT[:, a:b], op=ALU.mult),
                 rd=[bk, ("cs",)], wr=[("t1", bi)], group="SC")
            if c % 2 == 1 and bi == 1:
                ensure(JL1[s][("q", c // 2)] + 3)

        def r_blk(c, bi):
            a, b = qblocks[bi]
            qs = c % 2
            rb = next_acc(ACC1)
            rk = ("ps", id(rb))
            P.op("pe", lambda e: e.matmul(rb[:, :], swp[:, :], kb[:, bi * 512:(bi + 1) * 512], start=True, stop=True),
                 rd=[("kb", bi), ("c", "swp")], wr=[rk])
            P.op("dve", lambda e: e.tensor_tensor(out=t2[:, :], in0=rb[:, :], in1=sinT[:, a:b], op=ALU.mult),
                 rd=[rk, ("cs",)], wr=[("t2",)], group="SC")
            P.op("pool", lambda e: e.tensor_tensor(out=qr[qs][:, bi * 512:(bi + 1) * 512], in0=t1[:, bi * 512:(bi + 1) * 512], in1=t2[:, :], op=ALU.add),
                 rd=[("t1", bi), ("t2",)], wr=[("qr", qs, bi)], group="SC")

        def z_blk(c, bi):
            _, sz_ = slabs(c)
            co = (c % 2) * 128
            a, b = qblocks[bi]
            qs = c % 2
            bank = next_acc(ACC1)
            bk = ("ps", id(bank))
            mm_group2(bank[:, 0:512], [(WS3[sz_][:, kc, co:co + 128], Y3[:, kc, a:b]) for kc in range(KC)],
                      rd=[("ws", sz_)] + ykeys(a, b), wr=[bk])
            P.op("act", lambda e: e.activation(out=th[:, bi * 512:(bi + 1) * 512], in_=bank[:, :], func=AF.Tanh, scale=0.5),
                 rd=[bk], wr=[("th", bi)], group="SC")
            P.op("dve", lambda e: e.scalar_tensor_tensor(
                out=szq[qs][:, bi * 512:(bi + 1) * 512], in0=th[:, bi * 512:(bi + 1) * 512], scalar=1.0, in1=bank[:, :], op0=ALU.add, op1=ALU.mult),
                rd=[bk, ("th", bi)], wr=[("szq", qs, bi)], group="SC")

        def attn_steps(c):
            qs = c % 2
            kvh = c // 4
            stt = {"pt": {}, "od": None}

            def S(pi):
                kts = [k for k in (2 * pi, 2 * pi + 1) if k <= 8]
                banks = SCR[pi % 2]
                for kt in kts:
                    lo = max(kt - 1, 0) * 128
                    hi = min(kt + 1, 8) * 128
                    off = (kt % 2) * 256
                    a_ = off + (0 if kt >= 1 else 128)
                    b_ = off + (256 if kt <= 7 else 128)
                    for e_ in range(2):
                        bk = ("ps", id(banks[e_]))
                        qk = [("qr", qs, bi) for bi in range(2) if lo < (bi + 1) * 512 and hi > bi * 512]
                        ksel = 2 * (kvh // 2) + (0 if kvh % 2 == e_ else 1)
                        P.op("pe", lambda e, e_=e_, kt=kt, lo=lo, hi=hi, a_=a_, b_=b_, bank=banks[e_], ksel=ksel: e.matmul(
                            bank[:, a_:b_], KT2[64 * e_:64 * e_ + 64, ksel, kt * 128:(kt + 1) * 128],
                            qr[qs][64 * e_:64 * e_ + 64, lo:hi], start=True, stop=True), rd=[("KT", ksel, ya) for (ya, yb_) in yblocks if ya < (kt + 1) * 128 and yb_ > kt * 128] + qk, wr=[bk])
                va = 128 if pi == 0 else 0
                vb = 128 if pi == 4 else 512
                ri = pt_ring["i"] % 3
                pt_ring["i"] += 1
                stt["pt"][pi] = ri
                for e_ in range(2):
                    bk = ("ps", id(banks[e_]))
                    pk = ("pt", ri, e_)
                    P.op("act", lambda e, e_=e_, bank=banks[e_]: e.activation(
                        out=ptb[ri][e_][:, va:vb], in_=bank[:, va:vb], func=AF.Exp, scale=0.125), rd=[bk], wr=[pk], group="SC")
                    if pi == 0:
                        segs = [(128, 256, 512), (256, 512, 256)]
                    else:
                        segs = [(va, vb, va)]
                    for (xa, xb_, mo) in segs:
                        P.op("pool", lambda e, e_=e_, xa=xa, xb_=xb_, mo=mo: e.tensor_tensor(
                            out=ptb[ri][e_][:, xa:xb_], in0=ptb[ri][e_][:, xa:xb_], in1=m2[:, mo:mo + xb_ - xa], op=ALU.mult),
                            rd=[pk, ("c", "m2")], wr=[pk], group="SC")

            def PV(pi):
                for qt in (2 * pi, 2 * pi + 1):
                    if qt < 1 or qt > 8:
                        continue
                    r = (qt - 1) % 2
                    if r == 0:
                        stt["od"] = ODB[od_ring["i"] % 2]
                        od_ring["i"] += 1
                    od = stt["od"]
                    odk = ("ps", id(od))
                    pprev = stt["pt"][(qt - 1) // 2]
                    pcur = stt["pt"][qt // 2]
                    for (dst_col, lfn) in ((r * 128, "v"), (256 + r * 128, "o")):
                        for step in range(2):
                            for e_ in range(2):
                                prev_ap = ptb[pprev][e_][:, ((qt - 1) % 2) * 256 + 128:((qt - 1) % 2) * 256 + 256]
                                cur_ap = ptb[pcur][e_][:, (qt % 2) * 256:(qt % 2) * 256 + 128]
                                kt_, rhs = ((qt - 1, prev_ap), (qt, cur_ap))[step]
                                rdk = [("pt", pprev, e_), ("pt", pcur, e_), ("V", qt - 1), ("V", qt)]
                                l_ap = V3[:, kt_, kvh * 64:(kvh + 1) * 64] if lfn == "v" else ones2[:, 0:64]
                                P.op("pe", lambda e, od=od, e_=e_, dst_col=dst_col, l_ap=l_ap, rhs=rhs, step=step: e.matmul(
                                    od[64 * e_:64 * e_ + 64, dst_col:dst_col + 128], l_ap, rhs, start=(step == 0), stop=(step == 1),
                                    tile_position=(0, 64 * e_)), rd=rdk + [("c", "ones")], wr=[odk])
                    if r == 1:
                        q0 = (qt - 2) * 128
                        zk = [("szq", qs, bi) for bi in range(2) if q0 < (bi + 1) * 512 and q0 + 256 > bi * 512]
                        P.op("act", lambda e, od=od: e.activation(out=rr[:, :], in_=od[:, 256:512], func=AF.Identity, bias=exps2[:, c:c + 1], scale=1.0),
                             rd=[odk, ("c", "exps")], wr=[("rr",)], group="SC")
                        P.op("dve", lambda e: e.reciprocal(out=rr[:, :], in_=rr[:, :]), rd=[("rr",)], wr=[("rr",)], group="SC")
                        P.op("dve", lambda e, q0=q0: e.tensor_tensor(out=wv[:, :], in0=rr[:, :], in1=szq[qs][:, q0:q0 + 256], op=ALU.mult),
                             rd=[("rr",)] + zk, wr=[("wv",)], group="SC")
                        P.op("dve", lambda e, od=od, q0=q0: e.tensor_tensor(out=G3[:, c, q0:q0 + 256], in0=od[:, 0:256], in1=wv[:, :], op=ALU.mult),
                             rd=[odk, ("wv",)], wr=[("G", c)], group="R1")
            return S, PV

        q_blk(0, 0)
        q_blk(0, 1)
        r_blk(0, 0)
        r_blk(0, 1)
        z_blk(0, 0)
        z_blk(0, 1)
        pend = {"pv4": None}
        for c in range(KC):
            if c % 4 == 0:
                P.op("pool", lambda e, n=c // 4: e.dma_start(out=WO3[:, :, n * 512:(n + 1) * 512],
                                                             in_=w_out1[:, n * 512:(n + 1) * 512].rearrange("(c p) n -> p c n", p=128)),
                     wr=[("WO", c // 4)], dma=("WO", c // 4))
            S_, PV_ = attn_steps(c)
            n_ = c + 1
            nop = lambda: None
            fq0 = fq1 = fr0 = fr1 = fz0 = fz1 = nop
            if n_ < KC:
                fq0 = lambda n_=n_: q_blk(n_, 0)
                fq1 = lambda n_=n_: q_blk(n_, 1)
                fr0 = lambda n_=n_: r_blk(n_, 0)
                fr1 = lambda n_=n_: r_blk(n_, 1)
                fz0 = lambda n_=n_: z_blk(n_, 0)
                fz1 = lambda n_=n_: z_blk(n_, 1)
            S_(0)
            fq0()
            S_(1)
            PV_(0)
            fq1()
            S_(2)
            fr0()
            PV_(1)
            fz0()
            S_(3)
            fr1()
            PV_(2)
            fz1()
            S_(4)
            PV_(3)
            PV_(4)

        if debug_stop in ("l1p1", "l1qz"):
            break
        if s + 1 < S and STAGE_X:
            yusers = P.users("Y")
            for t in range(NT):
                P.op("pool", lambda e, s=s, t=t: e.dma_start(out=Y[:, t * D:(t + 1) * D], in_=xin[s + 1, LB + t * 128:LB + (t + 1) * 128, :]),
                     wr=[("ystage", t)], dma=("ystage", t), extra=yusers)
            staged[0] = True
        phase2(1, G3, TO // 128, 0,
               lambda t, s=s: x1s[s, HL + t * 128:HL + (t + 1) * 128, :],
               lambda t, s=s: (out[s * TO + t * 128:s * TO + (t + 1) * 128, :], ("out", s, t)),
               False)

    fin0 = [o for o in P.q["sp"] if o.dma == ("hout", 0)]
    fin1 = [o for o in P.q["sp"] if o.dma == ("hout", 1)]
    P.op("sp", None, extra=fin0[-1:] + fin1[-1:])

    P.finalize()
    csem = {e: st.enter_context(nc.semaphore(f"cs_{e}")) for e in ("pe", "act", "dve", "pool", "sp")}
    dsem = {k: st.enter_context(nc.semaphore("ds_" + "_".join(str(x) for x in k))) for k in P.dma_cnt}
    with nc.Block() as block:
        @block.tensor
        def _(e):
            P.replay("pe", e, csem, dsem)

        @block.scalar
        def _(e):
            P.replay("act", e, csem, dsem)

        @block.vector
        def _(e):
            P.replay("dve", e, csem, dsem)

        @block.gpsimd
        def _(e):
            P.replay("pool", e, csem, dsem)

        @block.sync
        def _(e):
            P.replay("sp", e, csem, dsem)
    st.close()
    return nc


def host_inputs(x, ln_g, ln_b, a_w_in, a_w_group, a_scale, a_w_out, b_w_k, b_w_v, b_w_qg, b_sinks, b_w_out):
    f32 = np.float32
    x = np.asarray(x, f32)
    shared = {
        "w_in": np.ascontiguousarray(np.asarray(a_w_in, f32)[0]),
        "w_grp": np.ascontiguousarray(np.asarray(a_w_group, f32)[0]),
        "w_out0": np.ascontiguousarray(np.asarray(a_w_out, f32)[0]),
        "w_k": np.ascontiguousarray(np.asarray(b_w_k, f32)),
        "w_v": np.ascontiguousarray(np.asarray(b_w_v, f32)),
        "w_qg": np.ascontiguousarray(np.asarray(b_w_qg, f32)[0]),
        "w_out1": np.ascontiguousarray(np.asarray(b_w_out, f32)[0]),
        "lngb": np.ascontiguousarray(np.concatenate([np.asarray(ln_g, f32), np.asarray(ln_b, f32)], axis=1)),
        "scale_col": np.ascontiguousarray(np.asarray(a_scale, f32)[0].reshape(16, 128).T),
    }
    sk = np.asarray(b_sinks, f32)[0]
    pidx = np.arange(128)[:, None] // 64
    cidx = np.arange(16)[None, :]
    shared["sink_col"] = np.ascontiguousarray(sk[2 * cidx + pidx])
    bf = ml_dtypes.bfloat16
    shared["ident"] = np.eye(128, dtype=f32).astype(bf)
    m = np.arange(128)
    partner = np.where((m % 64) < 32, m + 32, m - 32)
    swp = np.zeros((128, 128), f32)
    swp[partner, m] = 1.0
    shared["swp"] = swp.astype(bf)
    shared["ones2"] = np.full((128, 64), 2.0, f32).astype(bf)
    hsw = np.zeros((128, 128), f32)
    hsw[(m + 64) % 128, m] = 1.0
    shared["hsw"] = hsw.astype(bf)
    triu = np.triu(np.ones((128, 128), f32))
    stril = np.tril(np.ones((128, 128), f32), -1)
    gen = np.concatenate([triu, stril, triu, stril], axis=1)
    d = np.arange(128) % 64
    fidx = (d % 32).astype(f32)
    inv_freq = (np.float32(10000.0) ** (-(2.0 * fidx) / np.float32(64.0))).astype(f32)
    sign = np.where(d < 32, -1.0, 1.0).astype(f32)
    wins = (2, 4, 8, 16)
    in_maps = []
    for core in range(NCORES):
        b, half = core // 2, core % 2
        xin = np.zeros((S, TC, D), f32)
        invc = np.zeros((S, 128, 64), f32)
        cs = np.zeros((S, 128, 2 * TY), f32)
        m2 = np.zeros((S, 128, 640), f32)
        for s in range(S):
            start = half * 2048 + s * TO
            lo = start - (LB + HL)
            src_lo = max(lo, 0)
            xin[s, src_lo - lo:, :] = x[b, src_lo:start + TO, :]
            for g, w in enumerate(wins):
                t = np.arange(16, dtype=f32)
                if start == 0:
                    invc[s, :, g * 16:(g + 1) * 16] = (1.0 / np.minimum(t + 1.0, float(w)))[None, :]
                else:
                    invc[s, :, g * 16:(g + 1) * 16] = 1.0 / w
            pos = (start - HL + np.arange(TY)).astype(f32)
            ang = (pos[None, :] * inv_freq[:, None]).astype(f32).astype(np.float64)
            cs[s, :, 0:TY] = np.cos(ang).astype(f32)
            cs[s, :, TY:] = (np.sin(ang) * sign[:, None]).astype(f32)
            m2[s, :, 0:512] = gen
            m2[s, :, 512:640] = np.zeros_like(stril) if start == 0 else stril
        mp = dict(shared)
        mp["xin"] = xin
        mp["invc"] = invc
        mp["cs"] = cs
        mp["m2"] = m2.astype(bf)
        in_maps.append(mp)
    return in_maps


_NC_CACHE = {}


def kernel(x, ln_g, ln_b, a_w_in, a_w_group, a_scale, a_w_out, b_w_k, b_w_v, b_w_qg, b_sinks, b_w_out):
    in_maps = host_inputs(x, ln_g, ln_b, a_w_in, a_w_group, a_scale, a_w_out, b_w_k, b_w_v, b_w_qg, b_sinks, b_w_out)
    if "nc" not in _NC_CACHE:
        _NC_CACHE["nc"] = build_program()
    nc = _NC_CACHE["nc"]
    res = run_bass_kernel_spmd(nc, in_maps, core_ids=list(range(NCORES)))
    outp = np.zeros((4, 4096, D), np.float32)
    for core in range(NCORES):
        b, half = core // 2, core % 2
        outp[b, half * 2048:(half + 1) * 2048, :] = np.asarray(res.results[core]["out"], np.float32).reshape(S * TO, D)
    return outp
```
